# Optimizing a Trainium2 kernel written in Bass

```python
import jax, jax.numpy as jnp
from jax import lax
import numpy as np

D_MODEL = 1024
BATCH = 8
SEQ = 2048
DEPTH = 4

CHUNK = 64
N_LEFT_CHUNKS = 8
HEAD_DIM = 64
D_ATT = 3 * D_MODEL // 8
D_RET = 3 * D_MODEL // 8
D_CONV = D_MODEL - D_ATT - D_RET
D_MIX = D_ATT + D_CONV + D_RET
H_ATT = D_ATT // HEAD_DIM
H_RET = D_RET // HEAD_DIM
CONV_K = 31
REL_CLIP = 128
D_FF = 4 * D_MODEL
D_PLE = 256
ROPE_THETA = 10000.0
EPS = 1e-6
SPLIT_SIZES = [D_ATT, D_ATT, D_ATT, 2 * D_CONV, D_RET, D_RET, D_RET, D_RET]
D_IN = sum(SPLIT_SIZES)

kernel_name = "hymba_style_chunk_causal_hybrid_trunk"


def rmsnorm(x, g):
    xf = x.astype(jnp.float32)
    y = xf * lax.rsqrt(jnp.mean(xf * xf, axis=-1, keepdims=True) + EPS)
    return (y * g.astype(jnp.float32)).astype(x.dtype)


def layernorm(x, g, b):
    xf = x.astype(jnp.float32)
    mu = jnp.mean(xf, axis=-1, keepdims=True)
    var = jnp.mean(jnp.square(xf - mu), axis=-1, keepdims=True)
    y = (xf - mu) * lax.rsqrt(var + EPS)
    return (y * g.astype(jnp.float32) + b.astype(jnp.float32)).astype(x.dtype)


def to_chunk_heads(t, n_heads):
    b, s, _ = t.shape
    return t.reshape(b, s // CHUNK, CHUNK, n_heads, HEAD_DIM).transpose(0, 3, 1, 2, 4)


def from_chunk_heads(t):
    b, h, nc, c, d = t.shape
    return t.transpose(0, 2, 3, 1, 4).reshape(b, nc * c, h * d)


def chunked_attention(q, k, v, qn_g, kn_g, rel_bias):
    q = rmsnorm(to_chunk_heads(q, H_ATT), qn_g)
    k = rmsnorm(to_chunk_heads(k, H_ATT), kn_g)
    v = to_chunk_heads(v, H_ATT)
    b, h, nc, c, d = q.shape
    band = N_LEFT_CHUNKS + 1
    pad = ((0, 0), (0, 0), (N_LEFT_CHUNKS, 0), (0, 0), (0, 0))
    idx = jnp.arange(nc)[:, None] + jnp.arange(band)[None, :]
    kb = jnp.pad(k, pad)[:, :, idx].reshape(b, h, nc, band * c, d)
    vb = jnp.pad(v, pad)[:, :, idx].reshape(b, h, nc, band * c, d)
    valid = jnp.repeat(idx >= N_LEFT_CHUNKS, c, axis=1)
    q_pos = N_LEFT_CHUNKS * c + jnp.arange(c)
    rel = q_pos[:, None] - jnp.arange(band * c)[None, :]
    bias = rel_bias.astype(jnp.float32)[:, jnp.clip(rel, -REL_CLIP, REL_CLIP) + REL_CLIP]
    scores = jnp.einsum('bhncd,bhnkd->bhnck', q, kb).astype(jnp.float32) * (d ** -0.5)
    scores = scores + bias[None, :, None]
    scores = jnp.where(valid[None, None, :, None, :], scores, -1e30)
    probs = jax.nn.softmax(scores, axis=-1).astype(v.dtype)
    out = jnp.einsum('bhnck,bhnkd->bhncd', probs, vb)
    return from_chunk_heads(out)


def conformer_conv(u, conv_w, conv_b, ln_g, ln_b, pw_w, pw_b):
    a, gate = jnp.split(u, 2, axis=-1)
    glu = a * jax.nn.sigmoid(gate)
    y = lax.conv_general_dilated(
        glu, conv_w[:, None, :].astype(glu.dtype), window_strides=(1,),
        padding=[(CONV_K - 1, 0)], dimension_numbers=('NWC', 'WIO', 'NWC'),
        feature_group_count=D_CONV)
    y = layernorm(y + conv_b, ln_g, ln_b)
    y = jax.nn.silu(y)
    return y @ pw_w + pw_b


def rope(t, cos, sin):
    t1, t2 = jnp.split(t, 2, axis=-1)
    c = cos[None, :, None, :]
    s = sin[None, :, None, :]
    return jnp.concatenate([t1 * c - t2 * s, t1 * s + t2 * c], axis=-1)


def retention(q, k, v, g, gn_g):
    b, s, _ = q.shape
    nc = s // CHUNK
    pos = jnp.arange(s, dtype=jnp.float32)
    inv_freq = ROPE_THETA ** (-jnp.arange(0, HEAD_DIM, 2, dtype=jnp.float32) / HEAD_DIM)
    ang = pos[:, None] * inv_freq[None, :]
    cos, sin = jnp.cos(ang), jnp.sin(ang)
    qf = rope(q.astype(jnp.float32).reshape(b, s, H_RET, HEAD_DIM), cos, sin) * (HEAD_DIM ** -0.5)
    kf = rope(k.astype(jnp.float32).reshape(b, s, H_RET, HEAD_DIM), cos, sin)
    qc = qf.reshape(b, nc, CHUNK, H_RET, HEAD_DIM).transpose(0, 3, 1, 2, 4)
    kc = kf.reshape(b, nc, CHUNK, H_RET, HEAD_DIM).transpose(0, 3, 1, 2, 4)
    vc = to_chunk_heads(v.astype(jnp.float32), H_RET)
    log_gamma = jnp.log(1.0 - 2.0 ** (-5.0 - jnp.arange(H_RET, dtype=jnp.float32)))
    n = jnp.arange(CHUNK, dtype=jnp.float32)
    dist = jnp.abs(n[:, None] - n[None, :])
    d_intra = jnp.exp(log_gamma[:, None, None] * dist[None])
    intra = jnp.einsum('bhncm,bhnme->bhnce',
                       jnp.einsum('bhncd,bhnmd->bhncm', qc, kc) * d_intra[None, :, None], vc)
    k_dec = kc * jnp.exp(log_gamma[:, None] * (CHUNK - 1 - n)[None, :])[None, :, None, :, None]
    kv = jnp.einsum('bhnmd,bhnme->bhnde', k_dec, vc)
    chunk_decay = jnp.exp(log_gamma * CHUNK)[None, :, None, None]

    def step(state, kv_c):
        return chunk_decay * state + kv_c, state

    init = jnp.zeros((b, H_RET, HEAD_DIM, HEAD_DIM), jnp.float32)
    _, prev = lax.scan(step, init, jnp.moveaxis(kv, 2, 0))
    prev = jnp.moveaxis(prev, 0, 2)
    q_dec = qc * jnp.exp(log_gamma[:, None] * (n + 1.0)[None, :])[None, :, None, :, None]
    cross = jnp.einsum('bhncd,bhnde->bhnce', q_dec, prev)
    out = intra + cross
    mu = jnp.mean(out, axis=-1, keepdims=True)
    var = jnp.mean(jnp.square(out - mu), axis=-1, keepdims=True)
    out = from_chunk_heads((out - mu) * lax.rsqrt(var + EPS)) * gn_g.astype(jnp.float32)
    return (jax.nn.silu(g.astype(jnp.float32)) * out).astype(q.dtype)


def setup_inputs(seed: int = 0) -> dict:
    key = jax.random.key(seed)
    ks = jax.random.split(key, 24)
    f32 = jnp.float32

    def nrm(k, shape, scale):
        return jax.random.normal(k, shape, f32) * scale

    def gain(k, shape):
        return 1.0 + 0.02 * jax.random.normal(k, shape, f32)

    L = DEPTH
    return {
        "x": nrm(ks[0], (BATCH, SEQ, D_MODEL), 1.0),
        "p": nrm(ks[1], (DEPTH, BATCH, SEQ, D_PLE), 1.0),
        "norm_mix_g": gain(ks[2], (L, D_MODEL)),
        "w_in": nrm(ks[3], (L, D_MODEL, D_IN), D_MODEL ** -0.5),
        "qn_g": gain(ks[4], (L, HEAD_DIM)),
        "kn_g": gain(ks[5], (L, HEAD_DIM)),
        "rel_bias": nrm(ks[6], (L, H_ATT, 2 * REL_CLIP + 1), 0.1),
        "conv_w": nrm(ks[7], (L, CONV_K, D_CONV), CONV_K ** -0.5),
        "conv_b": nrm(ks[8], (L, D_CONV), 0.02),
        "conv_ln_g": gain(ks[9], (L, D_CONV)),
        "conv_ln_b": nrm(ks[10], (L, D_CONV), 0.02),
        "conv_pw_w": nrm(ks[11], (L, D_CONV, D_CONV), D_CONV ** -0.5),
        "conv_pw_b": nrm(ks[12], (L, D_CONV), 0.02),
        "ret_gn_g": gain(ks[13], (L, D_RET)),
        "w_o": nrm(ks[14], (L, D_MIX, D_MODEL), D_MIX ** -0.5),
        "norm_ffn_g": gain(ks[15], (L, D_MODEL)),
        "w1": nrm(ks[16], (L, D_MODEL, D_FF), D_MODEL ** -0.5),
        "w2": nrm(ks[17], (L, D_FF, D_MODEL), D_FF ** -0.5),
        "norm_ple_g": gain(ks[18], (L, D_MODEL)),
        "w_pg": nrm(ks[19], (L, D_MODEL, D_MODEL), D_MODEL ** -0.5),
        "w_ple": nrm(ks[20], (L, D_PLE, D_MODEL), D_PLE ** -0.5),
    }


def reference(x, p, norm_mix_g, w_in, qn_g, kn_g, rel_bias, conv_w, conv_b, conv_ln_g,
              conv_ln_b, conv_pw_w, conv_pw_b, ret_gn_g, w_o, norm_ffn_g, w1, w2,
              norm_ple_g, w_pg, w_ple):
    offsets = np.cumsum(SPLIT_SIZES)[:-1].tolist()
    h = x
    for i in range(DEPTH):
        xn = rmsnorm(h, norm_mix_g[i])
        proj = xn @ w_in[i]
        qa, ka, va, uc, qr, kr, vr, gr = jnp.split(proj, offsets, axis=-1)
        att = chunked_attention(qa, ka, va, qn_g[i], kn_g[i], rel_bias[i])
        conv = conformer_conv(uc, conv_w[i], conv_b[i], conv_ln_g[i], conv_ln_b[i],
                              conv_pw_w[i], conv_pw_b[i])
        ret = retention(qr, kr, vr, gr, ret_gn_g[i])
        h = h + jnp.concatenate([att, conv, ret], axis=-1) @ w_o[i]
        hn = rmsnorm(h, norm_ffn_g[i])
        h = h + jnp.square(jax.nn.relu(hn @ w1[i])) @ w2[i]
        gate = jax.nn.sigmoid(rmsnorm(h, norm_ple_g[i]) @ w_pg[i])
        h = h + gate * (p[i] @ w_ple[i])
    return h
```

```python
import numpy as np
import concourse.bass as bass
import concourse.mybir as mybir
from concourse.bass_utils import run_bass_kernel_spmd

F32 = mybir.dt.float32
BF16 = mybir.dt.bfloat16
AF = mybir.ActivationFunctionType
ALU = mybir.AluOpType

L_FULL = 4
D = 1024
S = 2048
NT = 4
TT = 512
DIN = 3200
DFF = 4096
DPLE = 256
CONVK = 31
EPS = 1e-6
NV = 99
SLOT = 4096
NSLOT = 3
NEG = -30000.0

C_IDENT = 0
C_BLK = 128
C_ONES = 256
C_ROPEC = 384
C_ROPES = C_ROPEC + S
C_DM = C_ROPES + S
C_DQ = C_DM + 6 * 128
C_DECV = C_DQ + 3 * 128
C_DEC128 = C_DECV + 6
NCONST = C_DEC128 + 3


class Sched:
    def __init__(self, nc):
        self.nc = nc
        self.eng = {"pe": nc.tensor, "act": nc.scalar, "dve": nc.vector,
                    "pool": nc.gpsimd, "sp": nc.sync}
        self.sem = {}
        self.cnt = {}
        for e in self.eng:
            self.sem["c_" + e] = nc.alloc_semaphore("c_" + e)
            self.cnt["c_" + e] = 0
        self.seen = {e: {} for e in self.eng}
        self.lastw = {}
        self.reads = {}
        self.scr_keys = set()
        self.n_wait = 0
        self.n_ins = 0

    def dma_sem(self, name):
        k = "d_" + name
        if k not in self.sem:
            self.sem[k] = self.nc.alloc_semaphore(k)
            self.cnt[k] = 0
        return k

    def _deps(self, e, reads, writes):
        need = {}

        def add(tok, same_ok):
            if tok is None:
                return
            sk, v = tok
            if same_ok and e == "pe" and sk == "c_pe":
                return
            if need.get(sk, 0) < v:
                need[sk] = v

        for r in reads:
            add(self.lastw.get(r), False)
        for w in writes:
            add(self.lastw.get(w), True)
            for t in self.reads.get(w, {}).items():
                add(t, True)
        eng = self.eng[e]
        seen = self.seen[e]
        for sk, v in need.items():
            if seen.get(sk, 0) >= v:
                continue
            eng.wait_ge(self.sem[sk], v)
            seen[sk] = v
            self.n_wait += 1

    def _commit(self, tok, reads, writes):
        for r in reads:
            d = self.reads.setdefault(r, {})
            if d.get(tok[0], 0) < tok[1]:
                d[tok[0]] = tok[1]
        for w in writes:
            self.lastw[w] = tok
            self.reads[w] = {}

    def op(self, e, fn, reads=(), writes=()):
        self._deps(e, reads, writes)
        ins = fn(self.eng[e])
        sk = "c_" + e
        self.cnt[sk] += 1
        ins.then_inc(self.sem[sk], 1)
        self.n_ins += 1
        self._commit((sk, self.cnt[sk]), reads, writes)

    def dma(self, q, semname, out, in_, reads=(), writes=()):
        sk = self.dma_sem(semname)
        self._deps(q, reads, writes)
        ins = self.eng[q].dma_start(out=out, in_=in_)
        self.cnt[sk] += 16
        ins.then_inc(self.sem[sk], 16)
        self.n_ins += 1
        self._commit((sk, self.cnt[sk]), reads, writes)

    def wait_all(self, e, keys):
        self._deps(e, list(keys), [])

    def phase(self, new_keys):
        merged = {}
        for k in self.scr_keys:
            for tok in [self.lastw.get(k)] + list(self.reads.get(k, {}).items()):
                if tok is not None and merged.get(tok[0], 0) < tok[1]:
                    merged[tok[0]] = tok[1]
        for k in new_keys:
            self.lastw[k] = None
            self.reads[k] = dict(merged)
            self.scr_keys.add(k)


def _pieces_for_layer(w_in, conv_pw_w, w_o, w1, w2, w_pg, w_ple):
    def kmaj(w):
        K, C = w.shape
        return np.ascontiguousarray(w.reshape(K // 128, 128, C).transpose(1, 0, 2)).reshape(128, -1)

    out = []
    for hp in range(3):
        cols = np.concatenate([np.arange(128) + o + hp * 128 for o in (0, 384, 768)])
        out.append(kmaj(w_in[:, cols]))
    out.append(kmaj(w_in[:, 1152:1664]))
    out.append(kmaj(conv_pw_w))
    d = np.arange(128)
    sw = (d // 64) * 64 + ((d % 64) + 32) % 64
    for hp in range(3):
        q = 1664 + hp * 128 + d
        qs = 1664 + hp * 128 + sw
        k = 2048 + hp * 128 + d
        ks = 2048 + hp * 128 + sw
        v = 2432 + hp * 128 + d
        g = 2816 + hp * 128 + d
        out.append(kmaj(w_in[:, np.concatenate([q, qs, k])]))
        out.append(kmaj(w_in[:, np.concatenate([ks, v, g])]))
    for half in range(2):
        out.append(kmaj(w_o[:, half * 512:(half + 1) * 512]))
    for g_ in range(8):
        out.append(kmaj(w1[:, g_ * 512:(g_ + 1) * 512]))
        out.append(kmaj(w2[g_ * 512:(g_ + 1) * 512, :]))
    for half in range(2):
        out.append(kmaj(w_ple[:, half * 512:(half + 1) * 512]))
        out.append(kmaj(w_pg[:, half * 512:(half + 1) * 512]))
    return out


def _consts():
    c = np.zeros((128, NCONST), np.float32)
    c[:, C_IDENT:C_IDENT + 128] = np.eye(128, dtype=np.float32)
    blk = np.zeros((128, 128), np.float32)
    blk[:64, :64] = 1.0
    blk[64:, 64:] = 1.0
    c[:, C_BLK:C_BLK + 128] = blk
    c[:, C_ONES:C_ONES + 128] = 1.0
    pos = np.arange(S, dtype=np.float32)
    inv_freq = (np.float32(10000.0) ** (-np.arange(0, 64, 2, dtype=np.float32) / np.float32(64))).astype(np.float32)
    ang = (pos[:, None] * inv_freq[None, :]).astype(np.float32)
    cos = np.cos(ang).astype(np.float32)
    sin = np.sin(ang).astype(np.float32)
    p = np.arange(128)
    dd = p % 64
    fi = dd % 32
    sign = np.where(dd < 32, -1.0, 1.0).astype(np.float32)
    c[:, C_ROPEC:C_ROPEC + S] = cos[:, fi].T
    c[:, C_ROPES:C_ROPES + S] = sin[:, fi].T * sign[:, None]
    lg = np.log(1.0 - 2.0 ** (-5.0 - np.arange(6, dtype=np.float64)))
    m = np.arange(128)[:, None]
    cc = np.arange(128)[None, :]
    for h in range(6):
        same = (m // 64) == (cc // 64)
        earlier = (m // 64) < (cc // 64)
        dm = np.where(same, np.exp(lg[h] * np.abs(cc - m)), np.where(earlier, np.exp(lg[h] * (cc - m)), 0.0)) / 8.0
        c[:, C_DM + h * 128:C_DM + (h + 1) * 128] = dm
        c[:, C_DECV + h] = np.exp(lg[h] * (127 - np.arange(128)))
    for hp in range(3):
        hh = 2 * hp + p // 64
        c[:, C_DQ + hp * 128:C_DQ + (hp + 1) * 128] = np.exp(lg[hh][:, None] * (np.arange(128)[None, :] + 1.0)) / 8.0
        c[:, C_DEC128 + hp] = np.exp(lg[hh] * 128.0)
    return c


def _bias_tables(rel_bias_l):
    pk = np.arange(128)[:, None]
    fq = np.arange(128)[None, :]
    out = np.empty((128, 3, 2, 5, 128), np.float32)
    for delta in range(5):
        rel = delta * 128 + fq - pk
        idx = np.clip(rel, -128, 128) + 128
        valid = np.ones((128, 128), bool)
        if delta == 4:
            valid = ~((pk < 64) & (fq >= 64))
        if delta == 0:
            valid = ~((pk >= 64) & (fq < 64))
        for h in range(6):
            tab = rel_bias_l[h][idx]
            out[:, h // 2, h % 2, delta, :] = np.where(valid, tab, np.float32(NEG))
    return out.reshape(128, -1)


def _vecs(l, I):
    v = np.zeros((128, NV), np.float32)
    p = np.arange(128)
    for j, name in enumerate(["norm_mix_g", "norm_ffn_g", "norm_ple_g"]):
        v[:, 8 * j:8 * j + 8] = I[name][l].reshape(8, 128).T
    v[:, 24] = I["qn_g"][l][p % 64]
    v[:, 25] = I["kn_g"][l][p % 64]
    v[:, 26:28] = I["conv_b"][l].reshape(2, 128).T
    v[:, 28:30] = I["conv_ln_g"][l].reshape(2, 128).T
    v[:, 30:32] = I["conv_ln_b"][l].reshape(2, 128).T
    v[:, 32:34] = I["conv_pw_b"][l].reshape(2, 128).T
    v[:, 34:37] = I["ret_gn_g"][l].reshape(3, 128).T
    cw = I["conv_w"][l]
    for c in range(2):
        v[:, 37 + c * 31:37 + (c + 1) * 31] = cw[:, c * 128:(c + 1) * 128].T
    return v


STATS = {}

class _Stop(Exception):
    pass


def build_nc(n_layers, piece_sizes, debug=None, stop=None):
    nc = bass.Bass("TRN2", target_bir_lowering=False)
    Sx = Sched(nc)
    TOT = sum(piece_sizes)
    xT_d = nc.dram_tensor("xT", [128, 8 * S], F32, kind="ExternalInput").ap()
    pT_d = nc.dram_tensor("pT", [128, n_layers * 2 * S], F32, kind="ExternalInput").ap()
    ws_d = nc.dram_tensor("wstream", [128, TOT], F32, kind="ExternalInput").ap()
    vec_d = nc.dram_tensor("vecs", [128, n_layers * NV], F32, kind="ExternalInput").ap()
    bt_d = nc.dram_tensor("btab", [128, n_layers * 3 * 1280], F32, kind="ExternalInput").ap()
    cst_d = nc.dram_tensor("consts", [128, NCONST], F32, kind="ExternalInput").ap()
    out_d = nc.dram_tensor("outT", [128, 8 * S], F32, kind="ExternalOutput").ap()
    dbg_d = {}

    hT = nc.alloc_sbuf_tensor("hT", [128, 8, S], F32)
    xnT = nc.alloc_sbuf_tensor("xnT", [128, 8, S], BF16)
    mixT = nc.alloc_sbuf_tensor("mixT", [128, 8, S], BF16)
    wring = nc.alloc_sbuf_tensor("wring", [128, NSLOT, SLOT], BF16)
    vecs = nc.alloc_sbuf_tensor("vecs_sb", [128, n_layers * NV], F32)
    cst = nc.alloc_sbuf_tensor("cst", [128, C_ROPEC], F32)
    cst2 = nc.alloc_sbuf_tensor("cst2", [128, NCONST - C_DM], F32)
    cbf = nc.alloc_sbuf_tensor("cbf", [128, 384], BF16)
    small = nc.alloc_sbuf_tensor("small", [128, 16], F32)
    tmpf = [nc.alloc_sbuf_tensor(f"tmpf{i}", [128, TT], F32) for i in range(5)]
    SCR_F32 = 8448
    scr = nc.alloc_sbuf_tensor("scr", [128, SCR_F32], F32)
    ps = [nc.alloc_psum_tensor(f"ps{i}", [128, TT], F32) for i in range(8)]

    ident_f = cst[:, C_IDENT:C_IDENT + 128]
    blk_f = cst[:, C_BLK:C_BLK + 128]
    ones_f = cst[:, C_ONES:C_ONES + 128]
    ident_b = cbf[:, 0:128]
    blk_b = cbf[:, 128:256]
    ones_b = cbf[:, 256:384]
    DM0 = 0
    DQ0 = C_DQ - C_DM
    DECV0 = C_DECV - C_DM
    DEC1280 = C_DEC128 - C_DM
    eps_t = small[:, 0:1]

    psn = [0]
    reserved = set()

    def psum():
        while True:
            b = psn[0] % 8
            psn[0] += 1
            if b not in reserved:
                return b

    tfn = [0]

    def tmp():
        i = tfn[0] % len(tmpf)
        tfn[0] += 1
        return tmpf[i], ("tmpf", i)

    class Carver:
        def __init__(self):
            self.off = 0
            self.keys = []

        def f32(self, name, n):
            ap = scr[:, self.off:self.off + n]
            self.off += n
            assert self.off <= SCR_F32, (name, self.off)
            self.keys.append(name)
            return ap

        def bf16(self, name, n):
            assert n % 2 == 0
            ap = scr[:, self.off:self.off + n // 2].bitcast(BF16)
            self.off += n // 2
            assert self.off <= SCR_F32, (name, self.off)
            self.keys.append(name)
            return ap

    offs = np.concatenate([[0], np.cumsum(piece_sizes)]).astype(int)
    NP = len(piece_sizes)
    wstate = {"issued": 0}

    def w_issue(i):
        slot = i % NSLOT
        n = piece_sizes[i]
        Sx.dma("pool", f"w{slot}", wring[:, slot, 0:n], ws_d[:, offs[i]:offs[i] + n], writes=[("w", slot)])

    def w_get(i):
        while wstate["issued"] <= i:
            w_issue(wstate["issued"])
            wstate["issued"] += 1
        return wring[:, i % NSLOT, :], ("w", i % NSLOT)

    def w_done(i):
        while wstate["issued"] < NP and wstate["issued"] < i + 1 + NSLOT:
            w_issue(wstate["issued"])
            wstate["issued"] += 1

    Sx.dma("sp", "cin0", cst[:, :], cst_d[:, 0:C_ROPEC], writes=["cst"])
    Sx.dma("sp", "cin1", cst2[:, :], cst_d[:, C_DM:NCONST], writes=["cst2"])
    Sx.dma("sp", "cin2", vecs[:, :], vec_d, writes=["vecs"])
    for k in range(8):
        Sx.dma("sp", f"xin{k}", hT[:, k, :], xT_d[:, k * S:(k + 1) * S], writes=[("h", k, t) for t in range(NT)])
    Sx.op("dve", lambda e: e.tensor_copy(cbf[:, :], cst[:, 0:384]), reads=["cst"], writes=["cbf"])
    Sx.op("dve", lambda e: e.memset(small[:, 0:1], EPS), writes=["small"])
    for i in range(min(NSLOT, NP)):
        w_get(i)

    def hkeys(k, t):
        return ("h", k, t)

    def rstd(src, srckey, scale, bufs=None):
        if bufs is not None:
            (ln_, lnk), (rs_, rsk_) = bufs
            Sx.op("act", lambda e: e.activation(ln_[:, :], src, AF.Ln, bias=eps_t, scale=scale),
                  reads=[srckey, "small"], writes=[lnk])
            Sx.op("act", lambda e: e.activation(rs_[:, :], ln_[:, :], AF.Exp, scale=-0.5), reads=[lnk], writes=[rsk_])
            return rs_, rsk_
        ln_, lnk = tmp()
        Sx.op("act", lambda e: e.activation(ln_[:, :], src, AF.Ln, bias=eps_t, scale=scale),
              reads=[srckey, "small"], writes=[lnk])
        rs_, rsk_ = tmp()
        Sx.op("act", lambda e: e.activation(rs_[:, :], ln_[:, :], AF.Exp, scale=-0.5), reads=[lnk], writes=[rsk_])
        return rs_, rsk_

    def rmsnorm(vb, gcol):
        for t in range(NT):
            sl = slice(t * TT, (t + 1) * TT)
            b = psum()
            for k in range(8):
                sq, sqk = tmp()
                sqb = sq[:, 0:TT // 2].bitcast(BF16)
                Sx.op("act", lambda e, k=k, sqb=sqb: e.activation(sqb, hT[:, k, sl], AF.Square),
                      reads=[hkeys(k, t)], writes=[sqk])
                Sx.op("pe", lambda e, k=k, sqb=sqb, b=b: e.matmul(ps[b][:, :], ones_b, sqb, start=(k == 0), stop=(k == 7)),
                      reads=[sqk, "cbf"], writes=[("ps", b)])
            rs, rsk = rstd(ps[b][:, :], ("ps", b), 1.0 / D)
            for k in range(8):
                Sx.op("dve", lambda e, k=k, rs=rs: e.scalar_tensor_tensor(
                    xnT[:, k, sl], hT[:, k, sl], vecs[:, vb + gcol + k:vb + gcol + k + 1], rs[:, :], ALU.mult, ALU.mult),
                    reads=[hkeys(k, t), rsk, "vecs"], writes=[("xn", k, t)])

    nst = {}

    def norm_p1(t):
        sl = slice(t * TT, (t + 1) * TT)
        sq8 = nst["sq8"]
        for k in range(8):
            Sx.op("act", lambda e, k=k: e.activation(sq8[:, k * TT:(k + 1) * TT], hT[:, k, sl], AF.Square),
                  reads=[hkeys(k, t)], writes=[("sq8", k)])

    def norm_p2(t, vb, gcol):
        sl = slice(t * TT, (t + 1) * TT)
        sq8 = nst["sq8"]
        b = psum()
        for k in range(8):
            Sx.op("pe", lambda e, k=k, b=b: e.matmul(ps[b][:, :], ones_b, sq8[:, k * TT:(k + 1) * TT], start=(k == 0), stop=(k == 7)),
                  reads=[("sq8", k), "cbf"], writes=[("ps", b)])
        rs, rsk = rstd(ps[b][:, :], ("ps", b), 1.0 / D)
        for k in range(8):
            Sx.op("dve", lambda e, k=k, rs=rs: e.scalar_tensor_tensor(
                xnT[:, k, sl], hT[:, k, sl], vecs[:, vb + gcol + k:vb + gcol + k + 1], rs[:, :], ALU.mult, ALU.mult),
                reads=[hkeys(k, t), rsk, "vecs"], writes=[("xn", k, t)])

    def project(wap, wkey, col0, nchunks, src, srckey, t, nk=8, kstride=None):
        sl = slice(t * TT, (t + 1) * TT)
        banks = []
        for c in range(nchunks):
            b = psum()
            for k in range(nk):
                lhsT = wap[:, k * kstride + col0 + c * 128: k * kstride + col0 + (c + 1) * 128]
                Sx.op("pe", lambda e, b=b, lhsT=lhsT, k=k: e.matmul(ps[b][:, :], lhsT, src[:, k, sl], start=(k == 0), stop=(k == nk - 1)),
                      reads=[wkey, (srckey, k, t)], writes=[("ps", b)])
            banks.append(b)
        return banks

    dbgst = nc.alloc_sbuf_tensor("dbgst", [128, TT], F32) if debug else None

    def dump(name, ap, key, shape=None):
        if not debug or name not in debug:
            return
        d = nc.dram_tensor("dbg_" + name, [128, TT], F32, kind="ExternalOutput").ap()
        dbg_d[name] = d
        Sx.op("dve", lambda e: e.tensor_copy(dbgst[:, :], ap), reads=key, writes=["dbgst"])
        Sx.dma("sp", "dbg", d, dbgst[:, :], reads=["dbgst"], writes=["dbgo_" + name])

    pi = [0]

    def chk(name):
        if stop == name:
            raise _Stop()

    def layer_body(l):
        vb = l * NV
        if l == 0:
            rmsnorm(vb, 0)
        if l == 0:
            dump("xn", xnT[:, 3, 512:1024], [("xn", 3, 1)])
        chk('norm')
        Sx.op("dve", lambda e: e.tensor_scalar(small[:, 1:2], vecs[:, vb + 24:vb + 25], 0.125, None, ALU.mult),
              reads=["vecs"], writes=["gq8"])
        gq8 = small[:, 1:2]
        gk = vecs[:, vb + 25:vb + 26]

        for hp in range(3):
            cv = Carver()
            kT = cv.bf16("a_kT", S)
            vaug = cv.bf16("a_vaug", 16 * 2 * 66)
            qT = cv.bf16("a_qT", S)
            PTr = cv.bf16("a_PTr", 7 * 2 * 640)
            atok = [cv.bf16(f"a_atok{i}", 128) for i in range(2)]
            btb = cv.bf16("a_bt", 1280)
            rec = cv.f32("a_rec", 4)
            cv.keys += [("a_kT", t_) for t_ in range(NT)] + [("a_vaug", t_) for t_ in range(NT)]
            cv.keys += [("a_qT", t_) for t_ in range(NT)] + [("a_PT", s_, h_) for s_ in range(7) for h_ in range(2)]
            Sx.phase(cv.keys)
            vaug4 = vaug.rearrange("p (j h d) -> p j h d", j=16, h=2)
            Sx.dma("pool", "bt", btb, bt_d[:, (l * 3 + hp) * 1280:(l * 3 + hp + 1) * 1280], writes=["a_bt"])
            Sx.op("act", lambda e: e.activation(btb, btb, AF.Exp), reads=["a_bt"], writes=["a_bt"])
            Sx.op("dve", lambda e: e.memset(vaug4[:, :, :, 64:65], 1.0), writes=[("a_vaug", t_) for t_ in range(NT)])
            wap, wkey = w_get(pi[0])
            for t in range(NT):
                sl = slice(t * TT, (t + 1) * TT)
                bq, bk, bv = project(wap, wkey, 0, 3, xnT, "xn", t, kstride=384)
                vt_, vtk = tmp()
                vT = vt_[:, 0:TT // 2].bitcast(BF16)
                Sx.op("act", lambda e, vT=vT: e.copy(vT, ps[bv][:, :]), reads=[("ps", bv)], writes=[vtk])
                bt_ = psum()
                psb = ps[bt_][:, :].bitcast(BF16)
                for i in range(4):
                    Sx.op("pe", lambda e, i=i, psb=psb, vT=vT: e.transpose(psb[:, i * 128:(i + 1) * 128], vT[:, i * 128:(i + 1) * 128], ident_b),
                          reads=[vtk, "cbf"], writes=[("ps", bt_)])
                Sx.op("dve", lambda e, psb=psb: e.tensor_copy(
                    vaug4[:, 4 * t:4 * t + 4, :, 0:64],
                    psb[:, 0:512].rearrange("p (j h d) -> p j h d", j=4, h=2)),
                    reads=[("ps", bt_)], writes=[("a_vaug", t)])
                for which, b, dst, dkey, gain in (("q", bq, qT[:, sl], ("a_qT", t), gq8), ("k", bk, kT[:, sl], ("a_kT", t), gk)):
                    sq, sqk = tmp()
                    sqb = sq[:, 0:TT // 2].bitcast(BF16)
                    Sx.op("act", lambda e, b=b, sqb=sqb: e.activation(sqb, ps[b][:, :], AF.Square),
                          reads=[("ps", b)], writes=[sqk])
                    b2 = psum()
                    Sx.op("pe", lambda e, b2=b2, sqb=sqb: e.matmul(ps[b2][:, :], blk_b, sqb, start=True, stop=True),
                          reads=[sqk, "cbf"], writes=[("ps", b2)])
                    rs, rsk = rstd(ps[b2][:, :], ("ps", b2), 1.0 / 64)
                    Sx.op("dve", lambda e, b=b, dst=dst, gain=gain, rs=rs: e.scalar_tensor_tensor(
                        dst, ps[b][:, :], gain, rs[:, :], ALU.mult, ALU.mult),
                        reads=[("ps", b), rsk, "vecs", "gq8"], writes=[dkey])
            w_done(pi[0])
            pi[0] += 1

            def qk(kt):
                nq = min(5, 16 - kt)
                ncol = nq * 128
                n1 = min(512, ncol)
                slot = kt % 7
                for h in range(2):
                    hb = h * 64
                    PTs = PTr[:, (slot * 2 + h) * 640:(slot * 2 + h + 1) * 640]
                    pk = ("a_PT", slot, h)
                    qkeys = [("a_qT", t_) for t_ in range(kt // 4, min(NT - 1, (kt * 128 + ncol - 1) // TT) + 1)]
                    bA = psum()
                    Sx.op("pe", lambda e, bA=bA, hb=hb: e.matmul(
                        ps[bA][:, 0:n1], kT[hb:hb + 64, kt * 128:(kt + 1) * 128], qT[hb:hb + 64, kt * 128:kt * 128 + n1],
                        start=True, stop=True),
                        reads=[("a_kT", kt // 4)] + qkeys, writes=[("ps", bA)])
                    Sx.op("act", lambda e, bA=bA, PTs=PTs: e.activation(PTs[:, 0:n1], ps[bA][:, 0:n1], AF.Exp),
                          reads=[("ps", bA)], writes=[pk])
                    if ncol > 512:
                        bB = psum()
                        Sx.op("pe", lambda e, bB=bB, hb=hb: e.matmul(
                            ps[bB][:, 0:128], kT[hb:hb + 64, kt * 128:(kt + 1) * 128], qT[hb:hb + 64, kt * 128 + 512:kt * 128 + 640],
                            start=True, stop=True),
                            reads=[("a_kT", kt // 4)] + qkeys, writes=[("ps", bB)])
                        Sx.op("act", lambda e, bB=bB, PTs=PTs: e.activation(PTs[:, 512:640], ps[bB][:, 0:128], AF.Exp),
                              reads=[("ps", bB)], writes=[pk])

            def qk_mask(kt):
                ncol = min(5, 16 - kt) * 128
                slot = kt % 7
                for h in range(2):
                    PTs = PTr[:, (slot * 2 + h) * 640:(slot * 2 + h + 1) * 640]
                    pk = ("a_PT", slot, h)
                    Sx.op("dve", lambda e, PTs=PTs, h=h: e.tensor_tensor(
                        PTs[:, 0:ncol], PTs[:, 0:ncol], btb[:, h * 640:h * 640 + ncol], ALU.mult),
                        reads=[pk, "a_bt"], writes=[pk])

            st_ = {}

            def pv(j):
                t = j // 4
                jj = j % 4
                if jj == 0:
                    st_[t] = psum()
                    reserved.add(st_[t])
                kts = list(range(max(0, j - 4), j + 1))
                bo = psum()
                for h in range(2):
                    for ki, kt in enumerate(kts):
                        delta = j - kt
                        base = ((kt % 7) * 2 + h) * 640 + delta * 128
                        Sx.op("pe", lambda e, h=h, kt=kt, base=base, ki=ki: e.matmul(
                            ps[bo][:, h * 128:h * 128 + 65], PTr[:, base:base + 128],
                            vaug4[:, kt, h, 0:65], start=(ki == 0), stop=(ki == len(kts) - 1)),
                            reads=[("a_PT", kt % 7, h), ("a_vaug", kt // 4)], writes=[("ps", bo)])
                Sx.op("dve", lambda e: e.reciprocal(
                    rec[:, 0:2], ps[bo][:, :].rearrange("p (h d) -> p h d", h=4)[:, 0:2, 64]),
                    reads=[("ps", bo)], writes=["a_rec"])
                at = atok[j % 2]
                atk = f"a_atok{j % 2}"
                for h in range(2):
                    Sx.op("dve", lambda e, h=h: e.tensor_scalar(
                        at[:, h * 64:(h + 1) * 64], ps[bo][:, h * 128:h * 128 + 64], rec[:, h:h + 1], None, ALU.mult),
                        reads=[("ps", bo), "a_rec"], writes=[atk])

            def pv_tr(j):
                t = j // 4
                jj = j % 4
                bo_t = st_[t]
                psbo = ps[bo_t][:, :].bitcast(BF16)
                at = atok[j % 2]
                atk = f"a_atok{j % 2}"
                Sx.op("pe", lambda e: e.transpose(psbo[:, jj * 128:(jj + 1) * 128], at, ident_b),
                      reads=[atk, "cbf"], writes=[("ps", bo_t)])
                if jj == 3:
                    Sx.op("act", lambda e: e.copy(mixT[:, hp, t * TT:(t + 1) * TT], psbo[:, 0:512]),
                          reads=[("ps", bo_t)], writes=[("mix", hp, t)])
                    reserved.discard(bo_t)

            qk(0)
            qk_mask(0)
            qk(1)
            qk_mask(1)
            for kt in range(16):
                if kt + 2 < 16:
                    qk(kt + 2)
                pv(kt)
                if kt >= 1:
                    pv_tr(kt - 1)
                if kt + 2 < 16:
                    qk_mask(kt + 2)
            pv_tr(15)
        if l == 0:
            dump("att0", mixT[:, 0, 0:512], [("mix", 0, 0)])
            dump("att1", mixT[:, 1, 1024:1536], [("mix", 1, 2)])

        chk('att')
        cv = Carver()
        glu = cv.bf16("c_glu", 2 * (S + 32))
        diag = cv.bf16("c_diag", 2 * CONVK * 128)
        y32 = cv.f32("c_y32", 2 * TT)
        sbf = cv.bf16("c_s", 2 * TT)
        cv.keys += [("c_diag", c_, r_) for c_ in range(2) for r_ in range(3)]
        Sx.phase(cv.keys)
        glu3 = glu.rearrange("p (c n) -> p c n", c=2)
        Sx.op("dve", lambda e: e.memset(glu3[:, :, 0:30], 0.0), writes=["c_glu"])
        DG = [(0, 11), (11, 21), (21, 31)]
        for c in range(2):
            for gi, (j0, j1) in enumerate(DG):
                nj = j1 - j0
                dst_ = diag[:, (c * CONVK + j0) * 128:(c * CONVK + j1) * 128].rearrange("p (j q) -> p j q", j=nj)
                wv = vecs[:, vb + 37 + c * CONVK + j0:vb + 37 + c * CONVK + j1]
                Sx.op("dve", lambda e, dst_=dst_, wv=wv, nj=nj: e.tensor_tensor(
                    dst_, ident_f.unsqueeze(1).broadcast_to([128, nj, 128]),
                    wv.unsqueeze(2).broadcast_to([128, nj, 128]), ALU.mult),
                    reads=["cst", "vecs"], writes=[("c_diag", c, gi)])
        wap, wkey = w_get(pi[0])
        wpw, wpwkey = w_get(pi[0] + 1)
        for t in range(NT):
            sl = slice(t * TT, (t + 1) * TT)
            ba0, ba1, bg0, bg1 = project(wap, wkey, 0, 4, xnT, "xn", t, kstride=512)
            for c, (ba, bg) in enumerate(((ba0, bg0), (ba1, bg1))):
                sg, sgk = tmp()
                Sx.op("act", lambda e, bg=bg, sg=sg: e.activation(sg[:, :], ps[bg][:, :], AF.Sigmoid),
                      reads=[("ps", bg)], writes=[sgk])
                Sx.op("dve", lambda e, c=c, ba=ba, sg=sg: e.tensor_tensor(
                    glu3[:, c, 30 + t * TT:30 + (t + 1) * TT], ps[ba][:, :], sg[:, :], ALU.mult),
                    reads=[("ps", ba), sgk], writes=["c_glu"])
            by = []
            for c in range(2):
                b = psum()
                for jt in range(CONVK):
                    Sx.op("pe", lambda e, c=c, jt=jt, b=b: e.matmul(
                        ps[b][:, :], diag[:, (c * CONVK + jt) * 128:(c * CONVK + jt + 1) * 128],
                        glu3[:, c, t * TT + jt:t * TT + jt + TT], start=(jt == 0), stop=(jt == CONVK - 1)),
                        reads=[("c_diag", c, 0 if jt < 11 else (1 if jt < 21 else 2)), "c_glu"], writes=[("ps", b)])
                by.append(b)
            sqs = []
            for c in range(2):
                Sx.op("act", lambda e, c=c: e.activation(
                    y32[:, c * TT:(c + 1) * TT], ps[by[c]][:, :], AF.Identity, bias=vecs[:, vb + 26 + c:vb + 27 + c], scale=1.0),
                    reads=[("ps", by[c]), "vecs"], writes=["c_y32"])
                sq_, sqk_ = tmp()
                sqs.append((sq_, sqk_))
                Sx.op("act", lambda e, c=c, sq_=sq_: e.activation(
                    sq_[:, :], y32[:, c * TT:(c + 1) * TT], AF.Square),
                    reads=["c_y32"], writes=[sqk_])
            b1 = psum()
            b2 = psum()
            for c in range(2):
                Sx.op("pe", lambda e, c=c: e.matmul(ps[b1][:, :], ones_f, y32[:, c * TT:(c + 1) * TT], start=(c == 0), stop=(c == 1)),
                      reads=["c_y32", "cst"], writes=[("ps", b1)])
            for c in range(2):
                Sx.op("pe", lambda e, c=c: e.matmul(ps[b2][:, :], ones_f, sqs[c][0][:, :], start=(c == 0), stop=(c == 1)),
                      reads=[sqs[c][1], "cst"], writes=[("ps", b2)])
            mean, mk = tmp()
            Sx.op("dve", lambda e, mean=mean: e.tensor_scalar(mean[:, :], ps[b1][:, :], 1.0 / 256, None, ALU.mult),
                  reads=[("ps", b1)], writes=[mk])
            msq, msk = tmp()
            Sx.op("dve", lambda e, mean=mean, msq=msq: e.tensor_tensor(msq[:, :], mean[:, :], mean[:, :], ALU.mult),
                  reads=[mk], writes=[msk])
            var, vk = tmp()
            Sx.op("dve", lambda e, var=var, msq=msq: e.scalar_tensor_tensor(
                var[:, :], ps[b2][:, :], 1.0 / 256, msq[:, :], ALU.mult, ALU.subtract),
                reads=[("ps", b2), msk], writes=[vk])
            rstd(var[:, :], vk, 1.0, bufs=((msq, msk), (var, vk)))
            for c in range(2):
                ysl = y32[:, c * TT:(c + 1) * TT]
                Sx.op("dve", lambda e, ysl=ysl, mean=mean: e.tensor_tensor(ysl, ysl, mean[:, :], ALU.subtract),
                      reads=["c_y32", mk], writes=["c_y32"])
                Sx.op("dve", lambda e, ysl=ysl, var=var: e.tensor_tensor(ysl, ysl, var[:, :], ALU.mult),
                      reads=["c_y32", vk], writes=["c_y32"])
                Sx.op("act", lambda e, ysl=ysl, c=c: e.activation(
                    sbf[:, c * TT:(c + 1) * TT], ysl, AF.Silu, bias=vecs[:, vb + 30 + c:vb + 31 + c],
                    scale=vecs[:, vb + 28 + c:vb + 29 + c]),
                    reads=["c_y32", "vecs"], writes=["c_s"])
            for co in range(2):
                b = psum()
                for ci in range(2):
                    Sx.op("pe", lambda e, b=b, ci=ci, co=co: e.matmul(
                        ps[b][:, :], wpw[:, ci * 256 + co * 128:ci * 256 + (co + 1) * 128], sbf[:, ci * TT:(ci + 1) * TT],
                        start=(ci == 0), stop=(ci == 1)),
                        reads=[wpwkey, "c_s"], writes=[("ps", b)])
                Sx.op("act", lambda e, b=b, co=co: e.activation(
                    mixT[:, 3 + co, sl], ps[b][:, :], AF.Identity, bias=vecs[:, vb + 32 + co:vb + 33 + co], scale=1.0),
                    reads=[("ps", b), "vecs"], writes=[("mix", 3 + co, t)])
        w_done(pi[0])
        w_done(pi[0] + 1)
        pi[0] += 2
        if l == 0:
            dump("conv0", mixT[:, 3, 0:512], [("mix", 3, 0)])
            dump("conv1", mixT[:, 4, 512:1024], [("mix", 4, 1)])

        chk('conv')
        for hp in range(3):
            cv = Carver()
            qr = cv.bf16("r_qr", TT)
            kr = cv.bf16("r_kr", TT)
            qd = cv.bf16("r_qd", TT)
            vT = cv.bf16("r_vT", TT)
            gT = cv.bf16("r_gT", TT)
            ktok = cv.bf16("r_ktok", TT)
            vtok = cv.bf16("r_vtok", TT)
            vdec = cv.bf16("r_vdec", TT)
            AT = cv.bf16("r_AT", 2 * TT)
            st32 = [cv.f32(f"r_st{i}", 64) for i in range(2)]
            stbf = cv.bf16("r_stbf", 4 * 64)
            ropeC = [cv.f32(f"r_rc{i}", TT) for i in range(2)]
            ropeS = [cv.f32(f"r_rs{i}", TT) for i in range(2)]
            cv.keys += [("r_AT", i_, h_) for i_ in range(4) for h_ in range(2)] + [("r_stbf", i_) for i_ in range(4)]
            Sx.phase(cv.keys)
            wA, wAk = w_get(pi[0])
            wB, wBk = w_get(pi[0] + 1)
            Sx.op("dve", lambda e: e.memset(st32[0], 0.0), writes=["r_st0"])
            cur = 0
            dq = cst2[:, DQ0 + hp * 128:DQ0 + (hp + 1) * 128]
            for t in range(NT):
                sl = slice(t * TT, (t + 1) * TT)
                rc, rs_ = ropeC[t % 2], ropeS[t % 2]
                rck, rsk_ = f"r_rc{t % 2}", f"r_rs{t % 2}"
                Sx.dma("sp", f"ropec{t % 2}", rc, cst_d[:, C_ROPEC + t * TT:C_ROPEC + (t + 1) * TT], writes=[rck])
                Sx.dma("sp", f"ropes{t % 2}", rs_, cst_d[:, C_ROPES + t * TT:C_ROPES + (t + 1) * TT], writes=[rsk_])
                bq, bqs, bk = project(wA, wAk, 0, 3, xnT, "xn", t, kstride=384)
                bks, bv, bg = project(wB, wBk, 0, 3, xnT, "xn", t, kstride=384)
                for b_, bs_, dst, dk in ((bq, bqs, qr, "r_qr"), (bk, bks, kr, "r_kr")):
                    t1, t1k = tmp()
                    t2, t2k = tmp()
                    Sx.op("dve", lambda e, b_=b_, t1=t1: e.tensor_tensor(t1[:, :], ps[b_][:, :], rc, ALU.mult),
                          reads=[("ps", b_), rck], writes=[t1k])
                    Sx.op("dve", lambda e, bs_=bs_, t2=t2: e.tensor_tensor(t2[:, :], ps[bs_][:, :], rs_, ALU.mult),
                          reads=[("ps", bs_), rsk_], writes=[t2k])
                    Sx.op("dve", lambda e, t1=t1, t2=t2, dst=dst: e.tensor_tensor(dst, t1[:, :], t2[:, :], ALU.add),
                          reads=[t1k, t2k], writes=[dk])
                Sx.op("act", lambda e: e.copy(vT, ps[bv][:, :]), reads=[("ps", bv)], writes=["r_vT"])
                Sx.op("act", lambda e: e.activation(gT, ps[bg][:, :], AF.Silu), reads=[("ps", bg)], writes=["r_gT"])
                chk('r_proj')
                btv = psum()
                psv = ps[btv][:, :].bitcast(BF16)
                for i in range(4):
                    Sx.op("pe", lambda e, i=i: e.transpose(psv[:, i * 128:(i + 1) * 128], vT[:, i * 128:(i + 1) * 128], ident_b),
                          reads=["r_vT", "cbf"], writes=[("ps", btv)])
                bss = []
                for h in range(2):
                    bs = psum()
                    bss.append(bs)
                    for i in range(4):
                        Sx.op("pe", lambda e, i=i, h=h, bs=bs: e.matmul(
                            ps[bs][:, i * 128:(i + 1) * 128],
                            kr[h * 64:(h + 1) * 64, i * 128:(i + 1) * 128],
                            qr[h * 64:(h + 1) * 64, i * 128:(i + 1) * 128], start=True, stop=True),
                            reads=["r_kr", "r_qr"], writes=[("ps", bs)])
                btk = psum()
                psk = ps[btk][:, :].bitcast(BF16)
                for i in range(4):
                    Sx.op("pe", lambda e, i=i: e.transpose(psk[:, i * 128:(i + 1) * 128], kr[:, i * 128:(i + 1) * 128], ident_b),
                          reads=["r_kr", "cbf"], writes=[("ps", btk)])
                for h in range(2):
                    Sx.op("dve", lambda e, h=h: e.tensor_scalar(
                        vdec.rearrange("p (i c) -> p i c", i=4)[:, :, h * 64:(h + 1) * 64],
                        psv[:, 0:512].rearrange("p (i c) -> p i c", i=4)[:, :, h * 64:(h + 1) * 64],
                        cst2[:, DECV0 + 2 * hp + h:DECV0 + 2 * hp + h + 1], None, ALU.mult),
                        reads=[("ps", btv), "cst2"], writes=["r_vdec"])
                Sx.op("dve", lambda e: e.tensor_copy(vtok, psv[:, 0:512]), reads=[("ps", btv)], writes=["r_vtok"])
                Sx.op("act", lambda e: e.copy(ktok, psk[:, 0:512]), reads=[("ps", btk)], writes=["r_ktok"])
                Sx.op("act", lambda e, cur=cur: e.copy(stbf[:, 0:64], st32[cur]),
                      reads=[f"r_st{cur}"], writes=[("r_stbf", 0)])
                for i in range(4):
                    for h in range(2):
                        dmh = cst2[:, DM0 + (2 * hp + h) * 128:DM0 + (2 * hp + h + 1) * 128]
                        Sx.op("dve", lambda e, i=i, h=h, dmh=dmh: e.tensor_tensor(
                            AT[:, (i * 2 + h) * 128:(i * 2 + h + 1) * 128],
                            ps[bss[h]][:, i * 128:(i + 1) * 128], dmh, ALU.mult),
                            reads=[("ps", bss[h]), "cst2"], writes=[("r_AT", i, h)])
                Sx.op("dve", lambda e: e.tensor_tensor(
                    qd.rearrange("p (i c) -> p i c", i=4), qr.rearrange("p (i c) -> p i c", i=4),
                    dq.unsqueeze(1).broadcast_to([128, 4, 128]), ALU.mult),
                    reads=["r_qr", "cst2"], writes=["r_qd"])
                chk('r_tr')
                bkv = psum()
                for i in range(4):
                    for h in range(2):
                        Sx.op("pe", lambda e, i=i, h=h: e.matmul(
                            ps[bkv][h * 64:(h + 1) * 64, i * 64:(i + 1) * 64],
                            ktok[:, i * 128 + h * 64:i * 128 + (h + 1) * 64],
                            vdec[:, i * 128 + h * 64:i * 128 + (h + 1) * 64], start=True, stop=True),
                            reads=["r_ktok", "r_vdec"], writes=[("ps", bkv)])
                for i in range(4):
                    if i > 0:
                        Sx.op("act", lambda e, i=i, cur=cur: e.copy(stbf[:, i * 64:(i + 1) * 64], st32[cur]),
                              reads=[f"r_st{cur}"], writes=[("r_stbf", i)])
                    Sx.op("dve", lambda e, i=i, cur=cur: e.scalar_tensor_tensor(
                        st32[1 - cur], st32[cur], cst2[:, DEC1280 + hp:DEC1280 + hp + 1], ps[bkv][:, i * 64:(i + 1) * 64],
                        ALU.mult, ALU.add),
                        reads=[f"r_st{cur}", ("ps", bkv), "cst2"], writes=[f"r_st{1 - cur}"])
                    cur = 1 - cur
                chk('r_kv')
                chk('r_s')
                bo = psum()
                for i in range(4):
                    T = 4 * t + i
                    for h in range(2):
                        o = ps[bo][h * 64:(h + 1) * 64, i * 128:(i + 1) * 128]
                        Sx.op("pe", lambda e, o=o, i=i, h=h, T=T: e.matmul(
                            o, vtok[:, i * 128 + h * 64:i * 128 + (h + 1) * 64],
                            AT[:, (i * 2 + h) * 128:(i * 2 + h + 1) * 128], start=True, stop=(T == 0)),
                            reads=["r_vtok", ("r_AT", i, h)], writes=[("ps", bo)])
                        if T > 0:
                            Sx.op("pe", lambda e, o=o, i=i, h=h: e.matmul(
                                o, stbf[h * 64:(h + 1) * 64, i * 64:(i + 1) * 64],
                                qd[h * 64:(h + 1) * 64, i * 128:(i + 1) * 128], start=False, stop=True),
                                reads=[("r_stbf", i), "r_qd"], writes=[("ps", bo)])
                chk('r_o')
                o32, ok = tmp()
                s32, sk_ = tmp()
                Sx.op("act", lambda e, o32=o32: e.copy(o32[:, :], ps[bo][:, :]), reads=[("ps", bo)], writes=[ok])
                Sx.op("act", lambda e, s32=s32: e.activation(s32[:, :], ps[bo][:, :], AF.Square), reads=[("ps", bo)], writes=[sk_])
                b1 = psum()
                b2 = psum()
                Sx.op("pe", lambda e, o32=o32: e.matmul(ps[b1][:, :], blk_f, o32[:, :], start=True, stop=True),
                      reads=[ok, "cst"], writes=[("ps", b1)])
                Sx.op("pe", lambda e, s32=s32: e.matmul(ps[b2][:, :], blk_f, s32[:, :], start=True, stop=True),
                      reads=[sk_, "cst"], writes=[("ps", b2)])
                mean, mk = tmp()
                Sx.op("dve", lambda e, mean=mean: e.tensor_scalar(mean[:, :], ps[b1][:, :], 1.0 / 64, None, ALU.mult),
                      reads=[("ps", b1)], writes=[mk])
                Sx.op("dve", lambda e, mean=mean, s32=s32: e.tensor_tensor(s32[:, :], mean[:, :], mean[:, :], ALU.mult),
                      reads=[mk], writes=[sk_])
                var, vk = tmp()
                Sx.op("dve", lambda e, var=var, s32=s32: e.scalar_tensor_tensor(
                    var[:, :], ps[b2][:, :], 1.0 / 64, s32[:, :], ALU.mult, ALU.subtract),
                    reads=[("ps", b2), sk_], writes=[vk])
                rstd(var[:, :], vk, 1.0, bufs=((s32, sk_), (var, vk)))
                Sx.op("dve", lambda e, o32=o32, mean=mean: e.tensor_tensor(o32[:, :], o32[:, :], mean[:, :], ALU.subtract),
                      reads=[ok, mk], writes=[ok])
                Sx.op("dve", lambda e, o32=o32, var=var: e.tensor_tensor(o32[:, :], o32[:, :], var[:, :], ALU.mult),
                      reads=[ok, vk], writes=[ok])
                Sx.op("dve", lambda e, o32=o32: e.scalar_tensor_tensor(
                    mixT[:, 5 + hp, sl], o32[:, :], vecs[:, vb + 34 + hp:vb + 35 + hp], gT, ALU.mult, ALU.mult),
                    reads=[ok, "vecs", "r_gT"], writes=[("mix", 5 + hp, t)])
            w_done(pi[0])
            w_done(pi[0] + 1)
            pi[0] += 2
        if l == 0:
            dump("ret0", mixT[:, 5, 0:512], [("mix", 5, 0)])
            dump("ret1", mixT[:, 7, 512:1024], [("mix", 7, 1)])

        chk('ret')
        cv = Carver()
        hid = [cv.bf16(f"f_hid{i}", 4 * TT) for i in range(2)]
        pTs = cv.bf16("f_pT", 2 * S)
        nst["sq8"] = cv.bf16("f_sq8", 8 * TT)
        cv.keys += [("sq8", k_) for k_ in range(8)]
        Sx.phase(cv.keys)
        Sx.dma("pool", "pT", pTs, pT_d[:, l * 2 * S:(l + 1) * 2 * S], writes=["f_pT"])

        def wo_tile(wap, wkey, half, m, t):
            mo = half * 4 + m
            sl = slice(t * TT, (t + 1) * TT)
            b = psum()
            for k in range(8):
                Sx.op("pe", lambda e, b=b, k=k, m=m: e.matmul(
                    ps[b][:, :], wap[:, k * 512 + m * 128:k * 512 + (m + 1) * 128], mixT[:, k, sl],
                    start=(k == 0), stop=(k == 7)),
                    reads=[wkey, ("mix", k, t)], writes=[("ps", b)])
            Sx.op("dve", lambda e, b=b, mo=mo: e.tensor_tensor(hT[:, mo, sl], hT[:, mo, sl], ps[b][:, :], ALU.add),
                  reads=[("ps", b), hkeys(mo, t)], writes=[hkeys(mo, t)])

        wap, wkey = w_get(pi[0])
        for m in range(4):
            for t in range(NT):
                wo_tile(wap, wkey, 0, m, t)
        w_done(pi[0])
        pi[0] += 1
        wap, wkey = w_get(pi[0])
        for t in range(NT):
            for m in range(4):
                wo_tile(wap, wkey, 1, m, t)
            if t >= 1:
                norm_p2(t - 1, vb, 8)
            norm_p1(t)
        norm_p2(NT - 1, vb, 8)
        w_done(pi[0])
        pi[0] += 1
        if l == 0:
            dump("h1", hT[:, 2, 512:1024], [("h", 2, 1)])

        chk('wo')
        steps = [(g_, t) for g_ in range(8) for t in range(NT)]
        pbase = pi[0]

        def ffn_up(n):
            g_, t = steps[n]
            w1a, w1k = w_get(pbase + 2 * g_)
            hd = hid[n % 2]
            hdk = f"f_hid{n % 2}"
            banks = project(w1a, w1k, 0, 4, xnT, "xn", t, kstride=512)
            for c, b in enumerate(banks):
                r, rk = tmp()
                Sx.op("act", lambda e, b=b, r=r: e.activation(r[:, :], ps[b][:, :], AF.Relu), reads=[("ps", b)], writes=[rk])
                Sx.op("dve", lambda e, r=r, c=c, hd=hd: e.tensor_tensor(hd[:, c * TT:(c + 1) * TT], r[:, :], r[:, :], ALU.mult),
                      reads=[rk], writes=[hdk])
            if t == NT - 1:
                w_done(pbase + 2 * g_)

        def ffn_down(n):
            g_, t = steps[n]
            sl = slice(t * TT, (t + 1) * TT)
            w2a, w2k = w_get(pbase + 2 * g_ + 1)
            hd = hid[n % 2]
            hdk = f"f_hid{n % 2}"
            for m in range(8):
                b = psum()
                for c in range(4):
                    Sx.op("pe", lambda e, b=b, c=c, m=m, hd=hd: e.matmul(
                        ps[b][:, :], w2a[:, c * 1024 + m * 128:c * 1024 + (m + 1) * 128], hd[:, c * TT:(c + 1) * TT],
                        start=(c == 0), stop=(c == 3)),
                        reads=[w2k, hdk], writes=[("ps", b)])
                Sx.op("dve", lambda e, b=b, m=m: e.tensor_tensor(hT[:, m, sl], hT[:, m, sl], ps[b][:, :], ALU.add),
                      reads=[("ps", b), hkeys(m, t)], writes=[hkeys(m, t)])
            if t == NT - 1:
                w_done(pbase + 2 * g_ + 1)

        ffn_up(0)
        for n in range(len(steps)):
            if n + 1 < len(steps):
                ffn_up(n + 1)
            ffn_down(n)
            g_, t = steps[n]
            if g_ == 7:
                if t >= 1:
                    norm_p2(t - 1, vb, 16)
                norm_p1(t)
        norm_p2(NT - 1, vb, 16)
        pi[0] += 16
        if l == 0:
            dump("h2", hT[:, 2, 512:1024], [("h", 2, 1)])

        chk('ffn')
        pT3 = pTs.rearrange("p (k n) -> p k n", k=2)

        def ple_tile(wpl, wplk, wpg, wpgk, half, m, t):
            mo = half * 4 + m
            sl = slice(t * TT, (t + 1) * TT)
            bg_ = psum()
            for k in range(8):
                Sx.op("pe", lambda e, k=k, m=m, bg_=bg_: e.matmul(
                    ps[bg_][:, :], wpg[:, k * 512 + m * 128:k * 512 + (m + 1) * 128], xnT[:, k, sl],
                    start=(k == 0), stop=(k == 7)),
                    reads=[wpgk, ("xn", k, t)], writes=[("ps", bg_)])
            bp_ = psum()
            for k in range(2):
                Sx.op("pe", lambda e, k=k, m=m, bp_=bp_: e.matmul(
                    ps[bp_][:, :], wpl[:, k * 512 + m * 128:k * 512 + (m + 1) * 128], pT3[:, k, sl],
                    start=(k == 0), stop=(k == 1)),
                    reads=[wplk, "f_pT"], writes=[("ps", bp_)])
            sg, sgk = tmp()
            Sx.op("act", lambda e, sg=sg, bg_=bg_: e.activation(sg[:, :], ps[bg_][:, :], AF.Sigmoid),
                  reads=[("ps", bg_)], writes=[sgk])
            Sx.op("dve", lambda e, sg=sg, bp_=bp_: e.tensor_tensor(sg[:, :], sg[:, :], ps[bp_][:, :], ALU.mult),
                  reads=[sgk, ("ps", bp_)], writes=[sgk])
            Sx.op("dve", lambda e, sg=sg, mo=mo: e.tensor_tensor(hT[:, mo, sl], hT[:, mo, sl], sg[:, :], ALU.add),
                  reads=[sgk, hkeys(mo, t)], writes=[hkeys(mo, t)])

        wpl, wplk = w_get(pi[0])
        wpg, wpgk = w_get(pi[0] + 1)
        for m in range(4):
            for t in range(NT):
                ple_tile(wpl, wplk, wpg, wpgk, 0, m, t)
        w_done(pi[0])
        w_done(pi[0] + 1)
        pi[0] += 2
        wpl, wplk = w_get(pi[0])
        wpg, wpgk = w_get(pi[0] + 1)
        nxt = l + 1 < n_layers
        for t in range(NT):
            for m in range(4):
                ple_tile(wpl, wplk, wpg, wpgk, 1, m, t)
            if nxt:
                if t >= 1:
                    norm_p2(t - 1, (l + 1) * NV, 0)
                norm_p1(t)
        if nxt:
            norm_p2(NT - 1, (l + 1) * NV, 0)
        w_done(pi[0])
        w_done(pi[0] + 1)
        pi[0] += 2

    try:
        for l in range(n_layers):
            layer_body(l)
    except _Stop:
        pass

    okeys = []
    for k in range(8):
        Sx.dma("sp", "out", out_d[:, k * S:(k + 1) * S], hT[:, k, :], reads=[hkeys(k, t) for t in range(NT)], writes=[("o", k)])
        okeys.append(("o", k))
    okeys += ["dbgo_" + n for n in dbg_d]
    Sx.wait_all("sp", okeys)
    STATS['ins'] = Sx.n_ins
    STATS['waits'] = Sx.n_wait
    return nc


def prepare(I, n_layers):
    pieces = []
    for l in range(n_layers):
        pieces += _pieces_for_layer(I["w_in"][l], I["conv_pw_w"][l], I["w_o"][l], I["w1"][l], I["w2"][l],
                                    I["w_pg"][l], I["w_ple"][l])
    sizes = [p.shape[1] for p in pieces]
    wstream = np.ascontiguousarray(np.concatenate(pieces, axis=1))
    vecs = np.ascontiguousarray(np.concatenate([_vecs(l, I) for l in range(n_layers)], axis=1))
    btab = np.ascontiguousarray(np.concatenate([_bias_tables(I["rel_bias"][l]) for l in range(n_layers)], axis=1))
    consts = _consts()
    in_maps = []
    for b in range(8):
        xT = np.ascontiguousarray(I["x"][b].T.reshape(8, 128, S).transpose(1, 0, 2)).reshape(128, 8 * S)
        pT = np.ascontiguousarray(
            np.stack([I["p"][l, b].T.reshape(2, 128, S).transpose(1, 0, 2).reshape(128, 2 * S) for l in range(n_layers)], axis=1)
        ).reshape(128, n_layers * 2 * S)
        in_maps.append({"xT": xT, "pT": pT, "wstream": wstream, "vecs": vecs, "btab": btab, "consts": consts})
    return sizes, in_maps


def run(I, n_layers=L_FULL, debug=None, trace=False, stop=None):
    I = {k: np.asarray(v, dtype=np.float32) for k, v in I.items()}
    sizes, in_maps = prepare(I, n_layers)
    nc = build_nc(n_layers, sizes, debug=debug, stop=stop)
    res = run_bass_kernel_spmd(nc, in_maps, core_ids=list(range(8)), trace=trace)
    outs = []
    for b in range(8):
        o = res.results[b]["outT"].reshape(128, 8, S).transpose(1, 0, 2).reshape(D, S).T
        outs.append(o)
    out = np.ascontiguousarray(np.stack(outs, axis=0)).astype(np.float32)
    return out, res


def kernel(**inputs):
    out, _ = run(inputs, L_FULL)
    return out
```

```python
import numpy as np
import concourse.bass as bass
import concourse.mybir as mybir
from concourse.bass_utils import run_bass_kernel_spmd

F32 = mybir.dt.float32
BF16 = mybir.dt.bfloat16
AF = mybir.ActivationFunctionType
ALU = mybir.AluOpType

L_FULL = 4
D = 1024
S = 2048
NT = 4
TT = 512
DIN = 3200
DFF = 4096
DPLE = 256
CONVK = 31
EPS = 1e-6
NV = 99
SLOT = 4096
NSLOT = 3
NEG = -30000.0

C_IDENT = 0
C_BLK = 128
C_ONES = 256
C_ROPEC = 384
C_ROPES = C_ROPEC + S
C_DM = C_ROPES + S
C_DQ = C_DM + 6 * 128
C_DECV = C_DQ + 3 * 128
C_DEC128 = C_DECV + 6
NCONST = C_DEC128 + 3


class Sched:
    def __init__(self, nc):
        self.nc = nc
        self.eng = {"pe": nc.tensor, "act": nc.scalar, "dve": nc.vector,
                    "pool": nc.gpsimd, "sp": nc.sync}
        self.sem = {}
        self.cnt = {}
        for e in self.eng:
            self.sem["c_" + e] = nc.alloc_semaphore("c_" + e)
            self.cnt["c_" + e] = 0
        self.seen = {e: {} for e in self.eng}
        self.lastw = {}
        self.reads = {}
        self.scr_keys = set()
        self.n_wait = 0
        self.n_ins = 0

    def dma_sem(self, name):
        k = "d_" + name
        if k not in self.sem:
            self.sem[k] = self.nc.alloc_semaphore(k)
            self.cnt[k] = 0
        return k

    def _deps(self, e, reads, writes):
        need = {}

        def add(tok, same_ok):
            if tok is None:
                return
            sk, v = tok
            if same_ok and e == "pe" and sk == "c_pe":
                return
            if need.get(sk, 0) < v:
                need[sk] = v

        for r in reads:
            add(self.lastw.get(r), False)
        for w in writes:
            add(self.lastw.get(w), True)
            for t in self.reads.get(w, {}).items():
                add(t, True)
        eng = self.eng[e]
        seen = self.seen[e]
        for sk, v in need.items():
            if seen.get(sk, 0) >= v:
                continue
            eng.wait_ge(self.sem[sk], v)
            seen[sk] = v
            self.n_wait += 1

    def _commit(self, tok, reads, writes):
        for r in reads:
            d = self.reads.setdefault(r, {})
            if d.get(tok[0], 0) < tok[1]:
                d[tok[0]] = tok[1]
        for w in writes:
            self.lastw[w] = tok
            self.reads[w] = {}

    def op(self, e, fn, reads=(), writes=()):
        self._deps(e, reads, writes)
        ins = fn(self.eng[e])
        sk = "c_" + e
        self.cnt[sk] += 1
        ins.then_inc(self.sem[sk], 1)
        self.n_ins += 1
        self._commit((sk, self.cnt[sk]), reads, writes)

    def dma(self, q, semname, out, in_, reads=(), writes=()):
        sk = self.dma_sem(semname)
        self._deps(q, reads, writes)
        ins = self.eng[q].dma_start(out=out, in_=in_)
        self.cnt[sk] += 16
        ins.then_inc(self.sem[sk], 16)
        self.n_ins += 1
        self._commit((sk, self.cnt[sk]), reads, writes)

    def wait_all(self, e, keys):
        self._deps(e, list(keys), [])

    def phase(self, new_keys):
        merged = {}
        for k in self.scr_keys:
            for tok in [self.lastw.get(k)] + list(self.reads.get(k, {}).items()):
                if tok is not None and merged.get(tok[0], 0) < tok[1]:
                    merged[tok[0]] = tok[1]
        for k in new_keys:
            self.lastw[k] = None
            self.reads[k] = dict(merged)
            self.scr_keys.add(k)


def _pieces_for_layer(w_in, conv_pw_w, w_o, w1, w2, w_pg, w_ple):
    def kmaj(w):
        K, C = w.shape
        return np.ascontiguousarray(w.reshape(K // 128, 128, C).transpose(1, 0, 2)).reshape(128, -1)

    out = []
    for hp in range(3):
        cols = np.concatenate([np.arange(128) + o + hp * 128 for o in (0, 384, 768)])
        out.append(kmaj(w_in[:, cols]))
    out.append(kmaj(w_in[:, 1152:1664]))
    out.append(kmaj(conv_pw_w))
    d = np.arange(128)
    sw = (d // 64) * 64 + ((d % 64) + 32) % 64
    for hp in range(3):
        q = 1664 + hp * 128 + d
        qs = 1664 + hp * 128 + sw
        k = 2048 + hp * 128 + d
        ks = 2048 + hp * 128 + sw
        v = 2432 + hp * 128 + d
        g = 2816 + hp * 128 + d
        out.append(kmaj(w_in[:, np.concatenate([q, qs, k])]))
        out.append(kmaj(w_in[:, np.concatenate([ks, v, g])]))
    for half in range(2):
        out.append(kmaj(w_o[:, half * 512:(half + 1) * 512]))
    for g_ in range(8):
        out.append(kmaj(w1[:, g_ * 512:(g_ + 1) * 512]))
        out.append(kmaj(w2[g_ * 512:(g_ + 1) * 512, :]))
    for half in range(2):
        out.append(kmaj(w_ple[:, half * 512:(half + 1) * 512]))
        out.append(kmaj(w_pg[:, half * 512:(half + 1) * 512]))
    return out


def _consts():
    c = np.zeros((128, NCONST), np.float32)
    c[:, C_IDENT:C_IDENT + 128] = np.eye(128, dtype=np.float32)
    blk = np.zeros((128, 128), np.float32)
    blk[:64, :64] = 1.0
    blk[64:, 64:] = 1.0
    c[:, C_BLK:C_BLK + 128] = blk
    c[:, C_ONES:C_ONES + 128] = 1.0
    pos = np.arange(S, dtype=np.float32)
    inv_freq = (np.float32(10000.0) ** (-np.arange(0, 64, 2, dtype=np.float32) / np.float32(64))).astype(np.float32)
    ang = (pos[:, None] * inv_freq[None, :]).astype(np.float32)
    cos = np.cos(ang).astype(np.float32)
    sin = np.sin(ang).astype(np.float32)
    p = np.arange(128)
    dd = p % 64
    fi = dd % 32
    sign = np.where(dd < 32, -1.0, 1.0).astype(np.float32)
    c[:, C_ROPEC:C_ROPEC + S] = cos[:, fi].T
    c[:, C_ROPES:C_ROPES + S] = sin[:, fi].T * sign[:, None]
    lg = np.log(1.0 - 2.0 ** (-5.0 - np.arange(6, dtype=np.float64)))
    m = np.arange(128)[:, None]
    cc = np.arange(128)[None, :]
    for h in range(6):
        same = (m // 64) == (cc // 64)
        earlier = (m // 64) < (cc // 64)
        dm = np.where(same, np.exp(lg[h] * np.abs(cc - m)), np.where(earlier, np.exp(lg[h] * (cc - m)), 0.0)) / 8.0
        c[:, C_DM + h * 128:C_DM + (h + 1) * 128] = dm
        c[:, C_DECV + h] = np.exp(lg[h] * (127 - np.arange(128)))
    for hp in range(3):
        hh = 2 * hp + p // 64
        c[:, C_DQ + hp * 128:C_DQ + (hp + 1) * 128] = np.exp(lg[hh][:, None] * (np.arange(128)[None, :] + 1.0)) / 8.0
        c[:, C_DEC128 + hp] = np.exp(lg[hh] * 128.0)
    return c


def _bias_tables(rel_bias_l):
    pk = np.arange(128)[:, None]
    fq = np.arange(128)[None, :]
    out = np.empty((128, 3, 2, 5, 128), np.float32)
    for delta in range(5):
        rel = delta * 128 + fq - pk
        idx = np.clip(rel, -128, 128) + 128
        valid = np.ones((128, 128), bool)
        if delta == 4:
            valid = ~((pk < 64) & (fq >= 64))
        if delta == 0:
            valid = ~((pk >= 64) & (fq < 64))
        for h in range(6):
            tab = rel_bias_l[h][idx]
            out[:, h // 2, h % 2, delta, :] = np.where(valid, tab, np.float32(NEG))
    return out.reshape(128, -1)


def _vecs(l, I):
    v = np.zeros((128, NV), np.float32)
    p = np.arange(128)
    for j, name in enumerate(["norm_mix_g", "norm_ffn_g", "norm_ple_g"]):
        v[:, 8 * j:8 * j + 8] = I[name][l].reshape(8, 128).T
    v[:, 24] = I["qn_g"][l][p % 64]
    v[:, 25] = I["kn_g"][l][p % 64]
    v[:, 26:28] = I["conv_b"][l].reshape(2, 128).T
    v[:, 28:30] = I["conv_ln_g"][l].reshape(2, 128).T
    v[:, 30:32] = I["conv_ln_b"][l].reshape(2, 128).T
    v[:, 32:34] = I["conv_pw_b"][l].reshape(2, 128).T
    v[:, 34:37] = I["ret_gn_g"][l].reshape(3, 128).T
    cw = I["conv_w"][l]
    for c in range(2):
        v[:, 37 + c * 31:37 + (c + 1) * 31] = cw[:, c * 128:(c + 1) * 128].T
    return v


STATS = {}

class _Stop(Exception):
    pass


def build_nc(n_layers, piece_sizes, debug=None, stop=None):
    nc = bass.Bass("TRN2", target_bir_lowering=False)
    Sx = Sched(nc)
    TOT = sum(piece_sizes)
    xT_d = nc.dram_tensor("xT", [128, 8 * S], F32, kind="ExternalInput").ap()
    pT_d = nc.dram_tensor("pT", [128, n_layers * 2 * S], F32, kind="ExternalInput").ap()
    ws_d = nc.dram_tensor("wstream", [128, TOT], F32, kind="ExternalInput").ap()
    vec_d = nc.dram_tensor("vecs", [128, n_layers * NV], F32, kind="ExternalInput").ap()
    bt_d = nc.dram_tensor("btab", [128, n_layers * 3 * 1280], F32, kind="ExternalInput").ap()
    cst_d = nc.dram_tensor("consts", [128, NCONST], F32, kind="ExternalInput").ap()
    out_d = nc.dram_tensor("outT", [128, 8 * S], F32, kind="ExternalOutput").ap()
    dbg_d = {}

    hT = nc.alloc_sbuf_tensor("hT", [128, 8, S], F32)
    xnT = nc.alloc_sbuf_tensor("xnT", [128, 8, S], BF16)
    mixT = nc.alloc_sbuf_tensor("mixT", [128, 8, S], BF16)
    wring = nc.alloc_sbuf_tensor("wring", [128, NSLOT, SLOT], BF16)
    vecs = nc.alloc_sbuf_tensor("vecs_sb", [128, n_layers * NV], F32)
    cst = nc.alloc_sbuf_tensor("cst", [128, C_ROPEC], F32)
    cst2 = nc.alloc_sbuf_tensor("cst2", [128, NCONST - C_DM], F32)
    cbf = nc.alloc_sbuf_tensor("cbf", [128, 384], BF16)
    small = nc.alloc_sbuf_tensor("small", [128, 16], F32)
    tmpf = [nc.alloc_sbuf_tensor(f"tmpf{i}", [128, TT], F32) for i in range(5)]
    SCR_F32 = 8448
    scr = nc.alloc_sbuf_tensor("scr", [128, SCR_F32], F32)
    ps = [nc.alloc_psum_tensor(f"ps{i}", [128, TT], F32) for i in range(8)]

    ident_f = cst[:, C_IDENT:C_IDENT + 128]
    blk_f = cst[:, C_BLK:C_BLK + 128]
    ones_f = cst[:, C_ONES:C_ONES + 128]
    ident_b = cbf[:, 0:128]
    blk_b = cbf[:, 128:256]
    ones_b = cbf[:, 256:384]
    DM0 = 0
    DQ0 = C_DQ - C_DM
    DECV0 = C_DECV - C_DM
    DEC1280 = C_DEC128 - C_DM
    eps_t = small[:, 0:1]

    psn = [0]
    reserved = set()

    def psum():
        while True:
            b = psn[0] % 8
            psn[0] += 1
            if b not in reserved:
                return b

    tfn = [0]

    def tmp():
        i = tfn[0] % len(tmpf)
        tfn[0] += 1
        return tmpf[i], ("tmpf", i)

    class Carver:
        def __init__(self):
            self.off = 0
            self.keys = []

        def f32(self, name, n):
            ap = scr[:, self.off:self.off + n]
            self.off += n
            assert self.off <= SCR_F32, (name, self.off)
            self.keys.append(name)
            return ap

        def bf16(self, name, n):
            assert n % 2 == 0
            ap = scr[:, self.off:self.off + n // 2].bitcast(BF16)
            self.off += n // 2
            assert self.off <= SCR_F32, (name, self.off)
            self.keys.append(name)
            return ap

    offs = np.concatenate([[0], np.cumsum(piece_sizes)]).astype(int)
    NP = len(piece_sizes)
    wstate = {"issued": 0}

    def w_issue(i):
        slot = i % NSLOT
        n = piece_sizes[i]
        Sx.dma("pool", f"w{slot}", wring[:, slot, 0:n], ws_d[:, offs[i]:offs[i] + n], writes=[("w", slot)])

    def w_get(i):
        while wstate["issued"] <= i:
            w_issue(wstate["issued"])
            wstate["issued"] += 1
        return wring[:, i % NSLOT, :], ("w", i % NSLOT)

    def w_done(i):
        while wstate["issued"] < NP and wstate["issued"] < i + 1 + NSLOT:
            w_issue(wstate["issued"])
            wstate["issued"] += 1

    Sx.dma("sp", "cin0", cst[:, :], cst_d[:, 0:C_ROPEC], writes=["cst"])
    Sx.dma("sp", "cin1", cst2[:, :], cst_d[:, C_DM:NCONST], writes=["cst2"])
    Sx.dma("sp", "cin2", vecs[:, :], vec_d, writes=["vecs"])
    for k in range(8):
        Sx.dma("sp", f"xin{k}", hT[:, k, :], xT_d[:, k * S:(k + 1) * S], writes=[("h", k, t) for t in range(NT)])
    Sx.op("dve", lambda e: e.tensor_copy(cbf[:, :], cst[:, 0:384]), reads=["cst"], writes=["cbf"])
    Sx.op("dve", lambda e: e.memset(small[:, 0:1], EPS), writes=["small"])
    for i in range(min(NSLOT, NP)):
        w_get(i)

    def hkeys(k, t):
        return ("h", k, t)

    def rstd(src, srckey, scale, bufs=None):
        if bufs is not None:
            (ln_, lnk), (rs_, rsk_) = bufs
            Sx.op("act", lambda e: e.activation(ln_[:, :], src, AF.Ln, bias=eps_t, scale=scale),
                  reads=[srckey, "small"], writes=[lnk])
            Sx.op("act", lambda e: e.activation(rs_[:, :], ln_[:, :], AF.Exp, scale=-0.5), reads=[lnk], writes=[rsk_])
            return rs_, rsk_
        ln_, lnk = tmp()
        Sx.op("act", lambda e: e.activation(ln_[:, :], src, AF.Ln, bias=eps_t, scale=scale),
              reads=[srckey, "small"], writes=[lnk])
        rs_, rsk_ = tmp()
        Sx.op("act", lambda e: e.activation(rs_[:, :], ln_[:, :], AF.Exp, scale=-0.5), reads=[lnk], writes=[rsk_])
        return rs_, rsk_

    def rmsnorm(vb, gcol):
        for t in range(NT):
            sl = slice(t * TT, (t + 1) * TT)
            b = psum()
            for k in range(8):
                sq, sqk = tmp()
                sqb = sq[:, 0:TT // 2].bitcast(BF16)
                Sx.op("act", lambda e, k=k, sqb=sqb: e.activation(sqb, hT[:, k, sl], AF.Square),
                      reads=[hkeys(k, t)], writes=[sqk])
                Sx.op("pe", lambda e, k=k, sqb=sqb, b=b: e.matmul(ps[b][:, :], ones_b, sqb, start=(k == 0), stop=(k == 7)),
                      reads=[sqk, "cbf"], writes=[("ps", b)])
            rs, rsk = rstd(ps[b][:, :], ("ps", b), 1.0 / D)
            for k in range(8):
                Sx.op("dve", lambda e, k=k, rs=rs: e.scalar_tensor_tensor(
                    xnT[:, k, sl], hT[:, k, sl], vecs[:, vb + gcol + k:vb + gcol + k + 1], rs[:, :], ALU.mult, ALU.mult),
                    reads=[hkeys(k, t), rsk, "vecs"], writes=[("xn", k, t)])

    nst = {}

    def norm_p1(t):
        sl = slice(t * TT, (t + 1) * TT)
        sq8 = nst["sq8"]
        for k in range(8):
            Sx.op("act", lambda e, k=k: e.activation(sq8[:, k * TT:(k + 1) * TT], hT[:, k, sl], AF.Square),
                  reads=[hkeys(k, t)], writes=[("sq8", k)])

    def norm_p2(t, vb, gcol):
        sl = slice(t * TT, (t + 1) * TT)
        sq8 = nst["sq8"]
        b = psum()
        for k in range(8):
            Sx.op("pe", lambda e, k=k, b=b: e.matmul(ps[b][:, :], ones_b, sq8[:, k * TT:(k + 1) * TT], start=(k == 0), stop=(k == 7)),
                  reads=[("sq8", k), "cbf"], writes=[("ps", b)])
        rs, rsk = rstd(ps[b][:, :], ("ps", b), 1.0 / D)
        for k in range(8):
            Sx.op("dve", lambda e, k=k, rs=rs: e.scalar_tensor_tensor(
                xnT[:, k, sl], hT[:, k, sl], vecs[:, vb + gcol + k:vb + gcol + k + 1], rs[:, :], ALU.mult, ALU.mult),
                reads=[hkeys(k, t), rsk, "vecs"], writes=[("xn", k, t)])

    def project(wap, wkey, col0, nchunks, src, srckey, t, nk=8, kstride=None):
        sl = slice(t * TT, (t + 1) * TT)
        banks = []
        for c in range(nchunks):
            b = psum()
            for k in range(nk):
                lhsT = wap[:, k * kstride + col0 + c * 128: k * kstride + col0 + (c + 1) * 128]
                Sx.op("pe", lambda e, b=b, lhsT=lhsT, k=k: e.matmul(ps[b][:, :], lhsT, src[:, k, sl], start=(k == 0), stop=(k == nk - 1)),
                      reads=[wkey, (srckey, k, t)], writes=[("ps", b)])
            banks.append(b)
        return banks

    dbgst = nc.alloc_sbuf_tensor("dbgst", [128, TT], F32) if debug else None

    def dump(name, ap, key, shape=None):
        if not debug or name not in debug:
            return
        d = nc.dram_tensor("dbg_" + name, [128, TT], F32, kind="ExternalOutput").ap()
        dbg_d[name] = d
        Sx.op("dve", lambda e: e.tensor_copy(dbgst[:, :], ap), reads=key, writes=["dbgst"])
        Sx.dma("sp", "dbg", d, dbgst[:, :], reads=["dbgst"], writes=["dbgo_" + name])

    pi = [0]

    def chk(name):
        if stop == name:
            raise _Stop()

    def layer_body(l):
        vb = l * NV
        if l == 0:
            rmsnorm(vb, 0)
        if l == 0:
            dump("xn", xnT[:, 3, 512:1024], [("xn", 3, 1)])
        chk('norm')
        Sx.op("dve", lambda e: e.tensor_scalar(small[:, 1:2], vecs[:, vb + 24:vb + 25], 0.125, None, ALU.mult),
              reads=["vecs"], writes=["gq8"])
        gq8 = small[:, 1:2]
        gk = vecs[:, vb + 25:vb + 26]

        for hp in range(3):
            cv = Carver()
            kT = cv.bf16("a_kT", S)
            vaug = cv.bf16("a_vaug", 16 * 2 * 66)
            qT = cv.bf16("a_qT", S)
            PTr = cv.bf16("a_PTr", 7 * 2 * 640)
            atok = [cv.bf16(f"a_atok{i}", 128) for i in range(2)]
            btb = cv.bf16("a_bt", 1280)
            rec = cv.f32("a_rec", 4)
            cv.keys += [("a_kT", t_) for t_ in range(NT)] + [("a_vaug", t_) for t_ in range(NT)]
            cv.keys += [("a_qT", t_) for t_ in range(NT)] + [("a_PT", s_, h_) for s_ in range(7) for h_ in range(2)]
            Sx.phase(cv.keys)
            vaug4 = vaug.rearrange("p (j h d) -> p j h d", j=16, h=2)
            Sx.dma("pool", "bt", btb, bt_d[:, (l * 3 + hp) * 1280:(l * 3 + hp + 1) * 1280], writes=["a_bt"])
            Sx.op("act", lambda e: e.activation(btb, btb, AF.Exp), reads=["a_bt"], writes=["a_bt"])
            Sx.op("dve", lambda e: e.memset(vaug4[:, :, :, 64:65], 1.0), writes=[("a_vaug", t_) for t_ in range(NT)])
            wap, wkey = w_get(pi[0])
            for t in range(NT):
                sl = slice(t * TT, (t + 1) * TT)
                sqbs = []
                (bq,) = project(wap, wkey, 0, 1, xnT, "xn", t, kstride=384)
                for b in (bq,):
                    sq, sqk = tmp()
                    sqb = sq[:, 0:TT // 2].bitcast(BF16)
                    Sx.op("act", lambda e, b=b, sqb=sqb: e.activation(sqb, ps[b][:, :], AF.Square),
                          reads=[("ps", b)], writes=[sqk])
                    sqbs.append((sqb, sqk))
                (bk,) = project(wap, wkey, 128, 1, xnT, "xn", t, kstride=384)
                for b in (bk,):
                    sq, sqk = tmp()
                    sqb = sq[:, 0:TT // 2].bitcast(BF16)
                    Sx.op("act", lambda e, b=b, sqb=sqb: e.activation(sqb, ps[b][:, :], AF.Square),
                          reads=[("ps", b)], writes=[sqk])
                    sqbs.append((sqb, sqk))
                (bv,) = project(wap, wkey, 256, 1, xnT, "xn", t, kstride=384)
                vt_, vtk = tmp()
                vT = vt_[:, 0:TT // 2].bitcast(BF16)
                Sx.op("act", lambda e, vT=vT: e.copy(vT, ps[bv][:, :]), reads=[("ps", bv)], writes=[vtk])
                b2s = []
                for (sqb, sqk) in sqbs:
                    b2 = psum()
                    Sx.op("pe", lambda e, b2=b2, sqb=sqb: e.matmul(ps[b2][:, :], blk_b, sqb, start=True, stop=True),
                          reads=[sqk, "cbf"], writes=[("ps", b2)])
                    b2s.append(b2)
                bt_ = psum()
                psb = ps[bt_][:, :].bitcast(BF16)
                for i in range(4):
                    Sx.op("pe", lambda e, i=i, psb=psb, vT=vT: e.transpose(psb[:, i * 128:(i + 1) * 128], vT[:, i * 128:(i + 1) * 128], ident_b),
                          reads=[vtk, "cbf"], writes=[("ps", bt_)])
                for b, b2, dst, dkey, gain in ((bq, b2s[0], qT[:, sl], ("a_qT", t), gq8), (bk, b2s[1], kT[:, sl], ("a_kT", t), gk)):
                    rs, rsk = rstd(ps[b2][:, :], ("ps", b2), 1.0 / 64)
                    Sx.op("dve", lambda e, b=b, dst=dst, gain=gain, rs=rs: e.scalar_tensor_tensor(
                        dst, ps[b][:, :], gain, rs[:, :], ALU.mult, ALU.mult),
                        reads=[("ps", b), rsk, "vecs", "gq8"], writes=[dkey])
                Sx.op("dve", lambda e, psb=psb: e.tensor_copy(
                    vaug4[:, 4 * t:4 * t + 4, :, 0:64],
                    psb[:, 0:512].rearrange("p (j h d) -> p j h d", j=4, h=2)),
                    reads=[("ps", bt_)], writes=[("a_vaug", t)])
            w_done(pi[0])
            pi[0] += 1

            def qk(kt):
                nq = min(5, 16 - kt)
                ncol = nq * 128
                n1 = min(512, ncol)
                slot = kt % 7
                for h in range(2):
                    hb = h * 64
                    PTs = PTr[:, (slot * 2 + h) * 640:(slot * 2 + h + 1) * 640]
                    pk = ("a_PT", slot, h)
                    qkeys = [("a_qT", t_) for t_ in range(kt // 4, min(NT - 1, (kt * 128 + ncol - 1) // TT) + 1)]
                    bA = psum()
                    Sx.op("pe", lambda e, bA=bA, hb=hb: e.matmul(
                        ps[bA][:, 0:n1], kT[hb:hb + 64, kt * 128:(kt + 1) * 128], qT[hb:hb + 64, kt * 128:kt * 128 + n1],
                        start=True, stop=True),
                        reads=[("a_kT", kt // 4)] + qkeys, writes=[("ps", bA)])
                    Sx.op("act", lambda e, bA=bA, PTs=PTs: e.activation(PTs[:, 0:n1], ps[bA][:, 0:n1], AF.Exp),
                          reads=[("ps", bA)], writes=[pk])
                    if ncol > 512:
                        bB = psum()
                        Sx.op("pe", lambda e, bB=bB, hb=hb: e.matmul(
                            ps[bB][:, 0:128], kT[hb:hb + 64, kt * 128:(kt + 1) * 128], qT[hb:hb + 64, kt * 128 + 512:kt * 128 + 640],
                            start=True, stop=True),
                            reads=[("a_kT", kt // 4)] + qkeys, writes=[("ps", bB)])
                        Sx.op("act", lambda e, bB=bB, PTs=PTs: e.activation(PTs[:, 512:640], ps[bB][:, 0:128], AF.Exp),
                              reads=[("ps", bB)], writes=[pk])

            def qk_mask(kt):
                ncol = min(5, 16 - kt) * 128
                slot = kt % 7
                for h in range(2):
                    PTs = PTr[:, (slot * 2 + h) * 640:(slot * 2 + h + 1) * 640]
                    pk = ("a_PT", slot, h)
                    Sx.op("dve", lambda e, PTs=PTs, h=h: e.tensor_tensor(
                        PTs[:, 0:ncol], PTs[:, 0:ncol], btb[:, h * 640:h * 640 + ncol], ALU.mult),
                        reads=[pk, "a_bt"], writes=[pk])

            st_ = {}

            def pv(j):
                t = j // 4
                jj = j % 4
                if jj == 0:
                    st_[t] = psum()
                    reserved.add(st_[t])
                kts = list(range(max(0, j - 4), j + 1))
                bo = psum()
                for h in range(2):
                    for ki, kt in enumerate(kts):
                        delta = j - kt
                        base = ((kt % 7) * 2 + h) * 640 + delta * 128
                        Sx.op("pe", lambda e, h=h, kt=kt, base=base, ki=ki: e.matmul(
                            ps[bo][:, h * 128:h * 128 + 65], PTr[:, base:base + 128],
                            vaug4[:, kt, h, 0:65], start=(ki == 0), stop=(ki == len(kts) - 1)),
                            reads=[("a_PT", kt % 7, h), ("a_vaug", kt // 4)], writes=[("ps", bo)])
                Sx.op("dve", lambda e: e.reciprocal(
                    rec[:, 0:2], ps[bo][:, :].rearrange("p (h d) -> p h d", h=4)[:, 0:2, 64]),
                    reads=[("ps", bo)], writes=["a_rec"])
                at = atok[j % 2]
                atk = f"a_atok{j % 2}"
                for h in range(2):
                    Sx.op("dve", lambda e, h=h: e.tensor_scalar(
                        at[:, h * 64:(h + 1) * 64], ps[bo][:, h * 128:h * 128 + 64], rec[:, h:h + 1], None, ALU.mult),
                        reads=[("ps", bo), "a_rec"], writes=[atk])

            def pv_tr(j):
                t = j // 4
                jj = j % 4
                bo_t = st_[t]
                psbo = ps[bo_t][:, :].bitcast(BF16)
                at = atok[j % 2]
                atk = f"a_atok{j % 2}"
                Sx.op("pe", lambda e: e.transpose(psbo[:, jj * 128:(jj + 1) * 128], at, ident_b),
                      reads=[atk, "cbf"], writes=[("ps", bo_t)])
                if jj == 3:
                    Sx.op("act", lambda e: e.copy(mixT[:, hp, t * TT:(t + 1) * TT], psbo[:, 0:512]),
                          reads=[("ps", bo_t)], writes=[("mix", hp, t)])
                    reserved.discard(bo_t)

            qk(0)
            qk_mask(0)
            qk(1)
            qk_mask(1)
            for kt in range(16):
                if kt + 2 < 16:
                    qk(kt + 2)
                pv(kt)
                if kt >= 1:
                    pv_tr(kt - 1)
                if kt + 2 < 16:
                    qk_mask(kt + 2)
            pv_tr(15)
        if l == 0:
            dump("att0", mixT[:, 0, 0:512], [("mix", 0, 0)])
            dump("att1", mixT[:, 1, 1024:1536], [("mix", 1, 2)])

        chk('att')
        cv = Carver()
        glu = cv.bf16("c_glu", 2 * (S + 32))
        diag = cv.bf16("c_diag", 2 * CONVK * 128)
        y32 = cv.f32("c_y32", 2 * TT)
        sbf = cv.bf16("c_s", 2 * TT)
        cv.keys += [("c_diag", c_, r_) for c_ in range(2) for r_ in range(3)]
        Sx.phase(cv.keys)
        glu3 = glu.rearrange("p (c n) -> p c n", c=2)
        Sx.op("dve", lambda e: e.memset(glu3[:, :, 0:30], 0.0), writes=["c_glu"])
        DG = [(0, 11), (11, 21), (21, 31)]
        for c in range(2):
            for gi, (j0, j1) in enumerate(DG):
                nj = j1 - j0
                dst_ = diag[:, (c * CONVK + j0) * 128:(c * CONVK + j1) * 128].rearrange("p (j q) -> p j q", j=nj)
                wv = vecs[:, vb + 37 + c * CONVK + j0:vb + 37 + c * CONVK + j1]
                Sx.op("dve", lambda e, dst_=dst_, wv=wv, nj=nj: e.tensor_tensor(
                    dst_, ident_f.unsqueeze(1).broadcast_to([128, nj, 128]),
                    wv.unsqueeze(2).broadcast_to([128, nj, 128]), ALU.mult),
                    reads=["cst", "vecs"], writes=[("c_diag", c, gi)])
        wap, wkey = w_get(pi[0])
        wpw, wpwkey = w_get(pi[0] + 1)
        cst_ = {}

        def conv_A(t):
            sl = slice(t * TT, (t + 1) * TT)
            ba0, ba1, bg0, bg1 = project(wap, wkey, 0, 4, xnT, "xn", t, kstride=512)
            cst_[("A", t)] = (ba0, ba1, bg0, bg1)

        def conv_GLU(t):
            ba0, ba1, bg0, bg1 = cst_[("A", t)]
            for c, (ba, bg) in enumerate(((ba0, bg0), (ba1, bg1))):
                sg, sgk = tmp()
                Sx.op("act", lambda e, bg=bg, sg=sg: e.activation(sg[:, :], ps[bg][:, :], AF.Sigmoid),
                      reads=[("ps", bg)], writes=[sgk])
                Sx.op("dve", lambda e, c=c, ba=ba, sg=sg: e.tensor_tensor(
                    glu3[:, c, 30 + t * TT:30 + (t + 1) * TT], ps[ba][:, :], sg[:, :], ALU.mult),
                    reads=[("ps", ba), sgk], writes=["c_glu"])

        def conv_B(t):
            by = []
            for c in range(2):
                b = psum()
                for jt in range(CONVK):
                    Sx.op("pe", lambda e, c=c, jt=jt, b=b: e.matmul(
                        ps[b][:, :], diag[:, (c * CONVK + jt) * 128:(c * CONVK + jt + 1) * 128],
                        glu3[:, c, t * TT + jt:t * TT + jt + TT], start=(jt == 0), stop=(jt == CONVK - 1)),
                        reads=[("c_diag", c, 0 if jt < 11 else (1 if jt < 21 else 2)), "c_glu"], writes=[("ps", b)])
                by.append(b)
            sqs = []
            for c in range(2):
                Sx.op("act", lambda e, c=c: e.activation(
                    y32[:, c * TT:(c + 1) * TT], ps[by[c]][:, :], AF.Identity, bias=vecs[:, vb + 26 + c:vb + 27 + c], scale=1.0),
                    reads=[("ps", by[c]), "vecs"], writes=["c_y32"])
                sq_, sqk_ = tmp()
                sqs.append((sq_, sqk_))
                Sx.op("act", lambda e, c=c, sq_=sq_: e.activation(
                    sq_[:, :], y32[:, c * TT:(c + 1) * TT], AF.Square),
                    reads=["c_y32"], writes=[sqk_])
            b1 = psum()
            b2 = psum()
            for c in range(2):
                Sx.op("pe", lambda e, c=c: e.matmul(ps[b1][:, :], ones_f, y32[:, c * TT:(c + 1) * TT], start=(c == 0), stop=(c == 1)),
                      reads=["c_y32", "cst"], writes=[("ps", b1)])
            for c in range(2):
                Sx.op("pe", lambda e, c=c: e.matmul(ps[b2][:, :], ones_f, sqs[c][0][:, :], start=(c == 0), stop=(c == 1)),
                      reads=[sqs[c][1], "cst"], writes=[("ps", b2)])
            cst_[("B", t)] = (b1, b2)

        def conv_CH(t):
            b1, b2 = cst_[("B", t)]
            mean, mk = tmp()
            Sx.op("dve", lambda e, mean=mean: e.tensor_scalar(mean[:, :], ps[b1][:, :], 1.0 / 256, None, ALU.mult),
                  reads=[("ps", b1)], writes=[mk])
            msq, msk = tmp()
            Sx.op("dve", lambda e, mean=mean, msq=msq: e.tensor_tensor(msq[:, :], mean[:, :], mean[:, :], ALU.mult),
                  reads=[mk], writes=[msk])
            var, vk = tmp()
            Sx.op("dve", lambda e, var=var, msq=msq: e.scalar_tensor_tensor(
                var[:, :], ps[b2][:, :], 1.0 / 256, msq[:, :], ALU.mult, ALU.subtract),
                reads=[("ps", b2), msk], writes=[vk])
            rstd(var[:, :], vk, 1.0, bufs=((msq, msk), (var, vk)))
            for c in range(2):
                ysl = y32[:, c * TT:(c + 1) * TT]
                Sx.op("dve", lambda e, ysl=ysl, mean=mean: e.tensor_tensor(ysl, ysl, mean[:, :], ALU.subtract),
                      reads=["c_y32", mk], writes=["c_y32"])
                Sx.op("dve", lambda e, ysl=ysl, var=var: e.tensor_tensor(ysl, ysl, var[:, :], ALU.mult),
                      reads=["c_y32", vk], writes=["c_y32"])
                Sx.op("act", lambda e, ysl=ysl, c=c: e.activation(
                    sbf[:, c * TT:(c + 1) * TT], ysl, AF.Silu, bias=vecs[:, vb + 30 + c:vb + 31 + c],
                    scale=vecs[:, vb + 28 + c:vb + 29 + c]),
                    reads=["c_y32", "vecs"], writes=["c_s"])

        def conv_PW(t):
            sl = slice(t * TT, (t + 1) * TT)
            for co in range(2):
                b = psum()
                for ci in range(2):
                    Sx.op("pe", lambda e, b=b, ci=ci, co=co: e.matmul(
                        ps[b][:, :], wpw[:, ci * 256 + co * 128:ci * 256 + (co + 1) * 128], sbf[:, ci * TT:(ci + 1) * TT],
                        start=(ci == 0), stop=(ci == 1)),
                        reads=[wpwkey, "c_s"], writes=[("ps", b)])
                Sx.op("act", lambda e, b=b, co=co: e.activation(
                    mixT[:, 3 + co, sl], ps[b][:, :], AF.Identity, bias=vecs[:, vb + 32 + co:vb + 33 + co], scale=1.0),
                    reads=[("ps", b), "vecs"], writes=[("mix", 3 + co, t)])

        conv_A(0)
        conv_GLU(0)
        for t in range(NT):
            conv_B(t)
            if t + 1 < NT:
                conv_A(t + 1)
            conv_CH(t)
            if t + 1 < NT:
                conv_GLU(t + 1)
            conv_PW(t)
        w_done(pi[0])
        w_done(pi[0] + 1)
        pi[0] += 2
        if l == 0:
            dump("conv0", mixT[:, 3, 0:512], [("mix", 3, 0)])
            dump("conv1", mixT[:, 4, 512:1024], [("mix", 4, 1)])

        chk('conv')
        for hp in range(3):
            cv = Carver()
            qr = cv.bf16("r_qr", TT)
            kr = cv.bf16("r_kr", TT)
            qd = cv.bf16("r_qd", TT)
            vT = cv.bf16("r_vT", TT)
            gT = cv.bf16("r_gT", TT)
            ktok = cv.bf16("r_ktok", TT)
            vtok = cv.bf16("r_vtok", TT)
            vdec = cv.bf16("r_vdec", TT)
            AT = cv.bf16("r_AT", 2 * TT)
            st32 = [cv.f32(f"r_st{i}", 64) for i in range(2)]
            stbf = cv.bf16("r_stbf", 4 * 64)
            ropeC = [cv.f32(f"r_rc{i}", TT) for i in range(2)]
            ropeS = [cv.f32(f"r_rs{i}", TT) for i in range(2)]
            cv.keys += [("r_AT", i_, h_) for i_ in range(4) for h_ in range(2)] + [("r_stbf", i_) for i_ in range(4)]
            Sx.phase(cv.keys)
            wA, wAk = w_get(pi[0])
            wB, wBk = w_get(pi[0] + 1)
            Sx.op("dve", lambda e: e.memset(st32[0], 0.0), writes=["r_st0"])
            cur = 0
            dq = cst2[:, DQ0 + hp * 128:DQ0 + (hp + 1) * 128]
            for t in range(NT):
                sl = slice(t * TT, (t + 1) * TT)
                rc, rs_ = ropeC[t % 2], ropeS[t % 2]
                rck, rsk_ = f"r_rc{t % 2}", f"r_rs{t % 2}"
                Sx.dma("sp", f"ropec{t % 2}", rc, cst_d[:, C_ROPEC + t * TT:C_ROPEC + (t + 1) * TT], writes=[rck])
                Sx.dma("sp", f"ropes{t % 2}", rs_, cst_d[:, C_ROPES + t * TT:C_ROPES + (t + 1) * TT], writes=[rsk_])
                bq, bqs, bk = project(wA, wAk, 0, 3, xnT, "xn", t, kstride=384)
                bks, bv, bg = project(wB, wBk, 0, 3, xnT, "xn", t, kstride=384)
                for b_, bs_, dst, dk in ((bq, bqs, qr, "r_qr"), (bk, bks, kr, "r_kr")):
                    t1, t1k = tmp()
                    t2, t2k = tmp()
                    Sx.op("dve", lambda e, b_=b_, t1=t1: e.tensor_tensor(t1[:, :], ps[b_][:, :], rc, ALU.mult),
                          reads=[("ps", b_), rck], writes=[t1k])
                    Sx.op("dve", lambda e, bs_=bs_, t2=t2: e.tensor_tensor(t2[:, :], ps[bs_][:, :], rs_, ALU.mult),
                          reads=[("ps", bs_), rsk_], writes=[t2k])
                    Sx.op("dve", lambda e, t1=t1, t2=t2, dst=dst: e.tensor_tensor(dst, t1[:, :], t2[:, :], ALU.add),
                          reads=[t1k, t2k], writes=[dk])
                Sx.op("act", lambda e: e.copy(vT, ps[bv][:, :]), reads=[("ps", bv)], writes=["r_vT"])
                Sx.op("act", lambda e: e.activation(gT, ps[bg][:, :], AF.Silu), reads=[("ps", bg)], writes=["r_gT"])
                chk('r_proj')
                btv = psum()
                psv = ps[btv][:, :].bitcast(BF16)
                for i in range(4):
                    Sx.op("pe", lambda e, i=i: e.transpose(psv[:, i * 128:(i + 1) * 128], vT[:, i * 128:(i + 1) * 128], ident_b),
                          reads=["r_vT", "cbf"], writes=[("ps", btv)])
                bss = []
                for h in range(2):
                    bs = psum()
                    bss.append(bs)
                    for i in range(4):
                        Sx.op("pe", lambda e, i=i, h=h, bs=bs: e.matmul(
                            ps[bs][:, i * 128:(i + 1) * 128],
                            kr[h * 64:(h + 1) * 64, i * 128:(i + 1) * 128],
                            qr[h * 64:(h + 1) * 64, i * 128:(i + 1) * 128], start=True, stop=True),
                            reads=["r_kr", "r_qr"], writes=[("ps", bs)])
                btk = psum()
                psk = ps[btk][:, :].bitcast(BF16)
                for i in range(4):
                    Sx.op("pe", lambda e, i=i: e.transpose(psk[:, i * 128:(i + 1) * 128], kr[:, i * 128:(i + 1) * 128], ident_b),
                          reads=["r_kr", "cbf"], writes=[("ps", btk)])
                for h in range(2):
                    Sx.op("dve", lambda e, h=h: e.tensor_scalar(
                        vdec.rearrange("p (i c) -> p i c", i=4)[:, :, h * 64:(h + 1) * 64],
                        psv[:, 0:512].rearrange("p (i c) -> p i c", i=4)[:, :, h * 64:(h + 1) * 64],
                        cst2[:, DECV0 + 2 * hp + h:DECV0 + 2 * hp + h + 1], None, ALU.mult),
                        reads=[("ps", btv), "cst2"], writes=["r_vdec"])
                Sx.op("dve", lambda e: e.tensor_copy(vtok, psv[:, 0:512]), reads=[("ps", btv)], writes=["r_vtok"])
                Sx.op("act", lambda e: e.copy(ktok, psk[:, 0:512]), reads=[("ps", btk)], writes=["r_ktok"])
                Sx.op("act", lambda e, cur=cur: e.copy(stbf[:, 0:64], st32[cur]),
                      reads=[f"r_st{cur}"], writes=[("r_stbf", 0)])
                for i in range(4):
                    for h in range(2):
                        dmh = cst2[:, DM0 + (2 * hp + h) * 128:DM0 + (2 * hp + h + 1) * 128]
                        Sx.op("dve", lambda e, i=i, h=h, dmh=dmh: e.tensor_tensor(
                            AT[:, (i * 2 + h) * 128:(i * 2 + h + 1) * 128],
                            ps[bss[h]][:, i * 128:(i + 1) * 128], dmh, ALU.mult),
                            reads=[("ps", bss[h]), "cst2"], writes=[("r_AT", i, h)])
                Sx.op("dve", lambda e: e.tensor_tensor(
                    qd.rearrange("p (i c) -> p i c", i=4), qr.rearrange("p (i c) -> p i c", i=4),
                    dq.unsqueeze(1).broadcast_to([128, 4, 128]), ALU.mult),
                    reads=["r_qr", "cst2"], writes=["r_qd"])
                chk('r_tr')
                bkv = psum()
                for i in range(4):
                    for h in range(2):
                        Sx.op("pe", lambda e, i=i, h=h: e.matmul(
                            ps[bkv][h * 64:(h + 1) * 64, i * 64:(i + 1) * 64],
                            ktok[:, i * 128 + h * 64:i * 128 + (h + 1) * 64],
                            vdec[:, i * 128 + h * 64:i * 128 + (h + 1) * 64], start=True, stop=True),
                            reads=["r_ktok", "r_vdec"], writes=[("ps", bkv)])
                for i in range(4):
                    if i > 0:
                        Sx.op("act", lambda e, i=i, cur=cur: e.copy(stbf[:, i * 64:(i + 1) * 64], st32[cur]),
                              reads=[f"r_st{cur}"], writes=[("r_stbf", i)])
                    Sx.op("dve", lambda e, i=i, cur=cur: e.scalar_tensor_tensor(
                        st32[1 - cur], st32[cur], cst2[:, DEC1280 + hp:DEC1280 + hp + 1], ps[bkv][:, i * 64:(i + 1) * 64],
                        ALU.mult, ALU.add),
                        reads=[f"r_st{cur}", ("ps", bkv), "cst2"], writes=[f"r_st{1 - cur}"])
                    cur = 1 - cur
                chk('r_kv')
                chk('r_s')
                bo = psum()
                for i in range(4):
                    T = 4 * t + i
                    for h in range(2):
                        o = ps[bo][h * 64:(h + 1) * 64, i * 128:(i + 1) * 128]
                        Sx.op("pe", lambda e, o=o, i=i, h=h, T=T: e.matmul(
                            o, vtok[:, i * 128 + h * 64:i * 128 + (h + 1) * 64],
                            AT[:, (i * 2 + h) * 128:(i * 2 + h + 1) * 128], start=True, stop=(T == 0)),
                            reads=["r_vtok", ("r_AT", i, h)], writes=[("ps", bo)])
                        if T > 0:
                            Sx.op("pe", lambda e, o=o, i=i, h=h: e.matmul(
                                o, stbf[h * 64:(h + 1) * 64, i * 64:(i + 1) * 64],
                                qd[h * 64:(h + 1) * 64, i * 128:(i + 1) * 128], start=False, stop=True),
                                reads=[("r_stbf", i), "r_qd"], writes=[("ps", bo)])
                chk('r_o')
                o32, ok = tmp()
                s32, sk_ = tmp()
                Sx.op("act", lambda e, o32=o32: e.copy(o32[:, :], ps[bo][:, :]), reads=[("ps", bo)], writes=[ok])
                Sx.op("act", lambda e, s32=s32: e.activation(s32[:, :], ps[bo][:, :], AF.Square), reads=[("ps", bo)], writes=[sk_])
                b1 = psum()
                b2 = psum()
                Sx.op("pe", lambda e, o32=o32: e.matmul(ps[b1][:, :], blk_f, o32[:, :], start=True, stop=True),
                      reads=[ok, "cst"], writes=[("ps", b1)])
                Sx.op("pe", lambda e, s32=s32: e.matmul(ps[b2][:, :], blk_f, s32[:, :], start=True, stop=True),
                      reads=[sk_, "cst"], writes=[("ps", b2)])
                mean, mk = tmp()
                Sx.op("dve", lambda e, mean=mean: e.tensor_scalar(mean[:, :], ps[b1][:, :], 1.0 / 64, None, ALU.mult),
                      reads=[("ps", b1)], writes=[mk])
                Sx.op("dve", lambda e, mean=mean, s32=s32: e.tensor_tensor(s32[:, :], mean[:, :], mean[:, :], ALU.mult),
                      reads=[mk], writes=[sk_])
                var, vk = tmp()
                Sx.op("dve", lambda e, var=var, s32=s32: e.scalar_tensor_tensor(
                    var[:, :], ps[b2][:, :], 1.0 / 64, s32[:, :], ALU.mult, ALU.subtract),
                    reads=[("ps", b2), sk_], writes=[vk])
                rstd(var[:, :], vk, 1.0, bufs=((s32, sk_), (var, vk)))
                Sx.op("dve", lambda e, o32=o32, mean=mean: e.tensor_tensor(o32[:, :], o32[:, :], mean[:, :], ALU.subtract),
                      reads=[ok, mk], writes=[ok])
                Sx.op("dve", lambda e, o32=o32, var=var: e.tensor_tensor(o32[:, :], o32[:, :], var[:, :], ALU.mult),
                      reads=[ok, vk], writes=[ok])
                Sx.op("dve", lambda e, o32=o32: e.scalar_tensor_tensor(
                    mixT[:, 5 + hp, sl], o32[:, :], vecs[:, vb + 34 + hp:vb + 35 + hp], gT, ALU.mult, ALU.mult),
                    reads=[ok, "vecs", "r_gT"], writes=[("mix", 5 + hp, t)])
            w_done(pi[0])
            w_done(pi[0] + 1)
            pi[0] += 2
        if l == 0:
            dump("ret0", mixT[:, 5, 0:512], [("mix", 5, 0)])
            dump("ret1", mixT[:, 7, 512:1024], [("mix", 7, 1)])

        chk('ret')
        cv = Carver()
        hid = [cv.bf16(f"f_hid{i}", 4 * TT) for i in range(2)]
        pTs = cv.bf16("f_pT", 2 * S)
        nst["sq8"] = cv.bf16("f_sq8", 8 * TT)
        cv.keys += [("sq8", k_) for k_ in range(8)]
        Sx.phase(cv.keys)
        Sx.dma("pool", "pT", pTs, pT_d[:, l * 2 * S:(l + 1) * 2 * S], writes=["f_pT"])

        def wo_tile(wap, wkey, half, m, t):
            mo = half * 4 + m
            sl = slice(t * TT, (t + 1) * TT)
            b = psum()
            for k in range(8):
                Sx.op("pe", lambda e, b=b, k=k, m=m: e.matmul(
                    ps[b][:, :], wap[:, k * 512 + m * 128:k * 512 + (m + 1) * 128], mixT[:, k, sl],
                    start=(k == 0), stop=(k == 7)),
                    reads=[wkey, ("mix", k, t)], writes=[("ps", b)])
            Sx.op("dve", lambda e, b=b, mo=mo: e.tensor_tensor(hT[:, mo, sl], hT[:, mo, sl], ps[b][:, :], ALU.add),
                  reads=[("ps", b), hkeys(mo, t)], writes=[hkeys(mo, t)])

        wap, wkey = w_get(pi[0])
        for m in range(4):
            for t in range(NT):
                wo_tile(wap, wkey, 0, m, t)
        w_done(pi[0])
        pi[0] += 1
        wap, wkey = w_get(pi[0])
        for t in range(NT):
            for m in range(4):
                wo_tile(wap, wkey, 1, m, t)
            if t >= 1:
                norm_p2(t - 1, vb, 8)
            norm_p1(t)
        norm_p2(NT - 1, vb, 8)
        w_done(pi[0])
        pi[0] += 1
        if l == 0:
            dump("h1", hT[:, 2, 512:1024], [("h", 2, 1)])

        chk('wo')
        steps = [(g_, t) for g_ in range(8) for t in range(NT)]
        pbase = pi[0]

        def ffn_up(n):
            g_, t = steps[n]
            w1a, w1k = w_get(pbase + 2 * g_)
            hd = hid[n % 2]
            hdk = f"f_hid{n % 2}"
            banks = project(w1a, w1k, 0, 4, xnT, "xn", t, kstride=512)
            for c, b in enumerate(banks):
                r, rk = tmp()
                Sx.op("act", lambda e, b=b, r=r: e.activation(r[:, :], ps[b][:, :], AF.Relu), reads=[("ps", b)], writes=[rk])
                Sx.op("dve", lambda e, r=r, c=c, hd=hd: e.tensor_tensor(hd[:, c * TT:(c + 1) * TT], r[:, :], r[:, :], ALU.mult),
                      reads=[rk], writes=[hdk])
            if t == NT - 1:
                w_done(pbase + 2 * g_)

        def ffn_down(n):
            g_, t = steps[n]
            sl = slice(t * TT, (t + 1) * TT)
            w2a, w2k = w_get(pbase + 2 * g_ + 1)
            hd = hid[n % 2]
            hdk = f"f_hid{n % 2}"
            for m in range(8):
                b = psum()
                for c in range(4):
                    Sx.op("pe", lambda e, b=b, c=c, m=m, hd=hd: e.matmul(
                        ps[b][:, :], w2a[:, c * 1024 + m * 128:c * 1024 + (m + 1) * 128], hd[:, c * TT:(c + 1) * TT],
                        start=(c == 0), stop=(c == 3)),
                        reads=[w2k, hdk], writes=[("ps", b)])
                Sx.op("dve", lambda e, b=b, m=m: e.tensor_tensor(hT[:, m, sl], hT[:, m, sl], ps[b][:, :], ALU.add),
                      reads=[("ps", b), hkeys(m, t)], writes=[hkeys(m, t)])
            if t == NT - 1:
                w_done(pbase + 2 * g_ + 1)

        ffn_up(0)
        for n in range(len(steps)):
            if n + 1 < len(steps):
                ffn_up(n + 1)
            ffn_down(n)
            g_, t = steps[n]
            if g_ == 7:
                if t >= 1:
                    norm_p2(t - 1, vb, 16)
                norm_p1(t)
        norm_p2(NT - 1, vb, 16)
        pi[0] += 16
        if l == 0:
            dump("h2", hT[:, 2, 512:1024], [("h", 2, 1)])

        chk('ffn')
        pT3 = pTs.rearrange("p (k n) -> p k n", k=2)

        def ple_tile(wpl, wplk, wpg, wpgk, half, m, t):
            mo = half * 4 + m
            sl = slice(t * TT, (t + 1) * TT)
            bg_ = psum()
            for k in range(8):
                Sx.op("pe", lambda e, k=k, m=m, bg_=bg_: e.matmul(
                    ps[bg_][:, :], wpg[:, k * 512 + m * 128:k * 512 + (m + 1) * 128], xnT[:, k, sl],
                    start=(k == 0), stop=(k == 7)),
                    reads=[wpgk, ("xn", k, t)], writes=[("ps", bg_)])
            bp_ = psum()
            for k in range(2):
                Sx.op("pe", lambda e, k=k, m=m, bp_=bp_: e.matmul(
                    ps[bp_][:, :], wpl[:, k * 512 + m * 128:k * 512 + (m + 1) * 128], pT3[:, k, sl],
                    start=(k == 0), stop=(k == 1)),
                    reads=[wplk, "f_pT"], writes=[("ps", bp_)])
            sg, sgk = tmp()
            Sx.op("act", lambda e, sg=sg, bg_=bg_: e.activation(sg[:, :], ps[bg_][:, :], AF.Sigmoid),
                  reads=[("ps", bg_)], writes=[sgk])
            Sx.op("dve", lambda e, sg=sg, bp_=bp_: e.tensor_tensor(sg[:, :], sg[:, :], ps[bp_][:, :], ALU.mult),
                  reads=[sgk, ("ps", bp_)], writes=[sgk])
            Sx.op("dve", lambda e, sg=sg, mo=mo: e.tensor_tensor(hT[:, mo, sl], hT[:, mo, sl], sg[:, :], ALU.add),
                  reads=[sgk, hkeys(mo, t)], writes=[hkeys(mo, t)])

        wpl, wplk = w_get(pi[0])
        wpg, wpgk = w_get(pi[0] + 1)
        for m in range(4):
            for t in range(NT):
                ple_tile(wpl, wplk, wpg, wpgk, 0, m, t)
        w_done(pi[0])
        w_done(pi[0] + 1)
        pi[0] += 2
        wpl, wplk = w_get(pi[0])
        wpg, wpgk = w_get(pi[0] + 1)
        nxt = l + 1 < n_layers
        for t in range(NT):
            for m in range(4):
                ple_tile(wpl, wplk, wpg, wpgk, 1, m, t)
            if nxt:
                if t >= 1:
                    norm_p2(t - 1, (l + 1) * NV, 0)
                norm_p1(t)
        if nxt:
            norm_p2(NT - 1, (l + 1) * NV, 0)
        w_done(pi[0])
        w_done(pi[0] + 1)
        pi[0] += 2

    try:
        for l in range(n_layers):
            layer_body(l)
    except _Stop:
        pass

    okeys = []
    for k in range(8):
        Sx.dma("sp", "out", out_d[:, k * S:(k + 1) * S], hT[:, k, :], reads=[hkeys(k, t) for t in range(NT)], writes=[("o", k)])
        okeys.append(("o", k))
    okeys += ["dbgo_" + n for n in dbg_d]
    Sx.wait_all("sp", okeys)
    STATS['ins'] = Sx.n_ins
    STATS['waits'] = Sx.n_wait
    return nc


def prepare(I, n_layers):
    pieces = []
    for l in range(n_layers):
        pieces += _pieces_for_layer(I["w_in"][l], I["conv_pw_w"][l], I["w_o"][l], I["w1"][l], I["w2"][l],
                                    I["w_pg"][l], I["w_ple"][l])
    sizes = [p.shape[1] for p in pieces]
    wstream = np.ascontiguousarray(np.concatenate(pieces, axis=1))
    vecs = np.ascontiguousarray(np.concatenate([_vecs(l, I) for l in range(n_layers)], axis=1))
    btab = np.ascontiguousarray(np.concatenate([_bias_tables(I["rel_bias"][l]) for l in range(n_layers)], axis=1))
    consts = _consts()
    in_maps = []
    for b in range(8):
        xT = np.ascontiguousarray(I["x"][b].T.reshape(8, 128, S).transpose(1, 0, 2)).reshape(128, 8 * S)
        pT = np.ascontiguousarray(
            np.stack([I["p"][l, b].T.reshape(2, 128, S).transpose(1, 0, 2).reshape(128, 2 * S) for l in range(n_layers)], axis=1)
        ).reshape(128, n_layers * 2 * S)
        in_maps.append({"xT": xT, "pT": pT, "wstream": wstream, "vecs": vecs, "btab": btab, "consts": consts})
    return sizes, in_maps


def run(I, n_layers=L_FULL, debug=None, trace=False, stop=None):
    I = {k: np.asarray(v, dtype=np.float32) for k, v in I.items()}
    sizes, in_maps = prepare(I, n_layers)
    nc = build_nc(n_layers, sizes, debug=debug, stop=stop)
    res = run_bass_kernel_spmd(nc, in_maps, core_ids=list(range(8)), trace=trace)
    outs = []
    for b in range(8):
        o = res.results[b]["outT"].reshape(128, 8, S).transpose(1, 0, 2).reshape(D, S).T
        outs.append(o)
    out = np.ascontiguousarray(np.stack(outs, axis=0)).astype(np.float32)
    return out, res


def kernel(**inputs):
    out, _ = run(inputs, L_FULL)
    return out
```

```python
import numpy as np
import concourse.bass as bass
import concourse.mybir as mybir
from concourse.bass_utils import run_bass_kernel_spmd

F32 = mybir.dt.float32
BF16 = mybir.dt.bfloat16
AF = mybir.ActivationFunctionType
ALU = mybir.AluOpType

L_FULL = 4
D = 1024
S = 2048
NT = 4
TT = 512
DIN = 3200
DFF = 4096
DPLE = 256
CONVK = 31
EPS = 1e-6
NV = 99
SLOT = 4096
NSLOT = 3
NEG = -30000.0

C_IDENT = 0
C_BLK = 128
C_ONES = 256
C_PERM = 384
C_ROPEC = 512
C_ROPES = C_ROPEC + S
C_DM = C_ROPES + S
C_DQ = C_DM + 6 * 128
C_DECV = C_DQ + 3 * 128
C_DEC128 = C_DECV + 6
NCONST = C_DEC128 + 3


class Sched:
    def __init__(self, nc):
        self.nc = nc
        self.eng = {"pe": nc.tensor, "act": nc.scalar, "dve": nc.vector,
                    "pool": nc.gpsimd, "sp": nc.sync}
        self.sem = {}
        self.cnt = {}
        for e in self.eng:
            self.sem["c_" + e] = nc.alloc_semaphore("c_" + e)
            self.cnt["c_" + e] = 0
        self.seen = {e: {} for e in self.eng}
        self.lastw = {}
        self.reads = {}
        self.scr_keys = set()
        self.n_wait = 0
        self.n_ins = 0

    def dma_sem(self, name):
        k = "d_" + name
        if k not in self.sem:
            self.sem[k] = self.nc.alloc_semaphore(k)
            self.cnt[k] = 0
        return k

    def _deps(self, e, reads, writes):
        need = {}

        def add(tok, same_ok):
            if tok is None:
                return
            sk, v = tok
            if same_ok and e == "pe" and sk == "c_pe":
                return
            if need.get(sk, 0) < v:
                need[sk] = v

        for r in reads:
            add(self.lastw.get(r), False)
        for w in writes:
            add(self.lastw.get(w), True)
            for t in self.reads.get(w, {}).items():
                add(t, True)
        eng = self.eng[e]
        seen = self.seen[e]
        for sk, v in need.items():
            if seen.get(sk, 0) >= v:
                continue
            eng.wait_ge(self.sem[sk], v)
            seen[sk] = v
            self.n_wait += 1

    def _commit(self, tok, reads, writes):
        for r in reads:
            d = self.reads.setdefault(r, {})
            if d.get(tok[0], 0) < tok[1]:
                d[tok[0]] = tok[1]
        for w in writes:
            self.lastw[w] = tok
            self.reads[w] = {}

    def op(self, e, fn, reads=(), writes=()):
        self._deps(e, reads, writes)
        ins = fn(self.eng[e])
        sk = "c_" + e
        self.cnt[sk] += 1
        ins.then_inc(self.sem[sk], 1)
        self.n_ins += 1
        self._commit((sk, self.cnt[sk]), reads, writes)

    def dma(self, q, semname, out, in_, reads=(), writes=()):
        sk = self.dma_sem(semname)
        self._deps(q, reads, writes)
        ins = self.eng[q].dma_start(out=out, in_=in_)
        self.cnt[sk] += 16
        ins.then_inc(self.sem[sk], 16)
        self.n_ins += 1
        self._commit((sk, self.cnt[sk]), reads, writes)

    def wait_all(self, e, keys):
        self._deps(e, list(keys), [])

    def phase(self, new_keys):
        merged = {}
        for k in self.scr_keys:
            for tok in [self.lastw.get(k)] + list(self.reads.get(k, {}).items()):
                if tok is not None and merged.get(tok[0], 0) < tok[1]:
                    merged[tok[0]] = tok[1]
        for k in new_keys:
            self.lastw[k] = None
            self.reads[k] = dict(merged)
            self.scr_keys.add(k)


def _pieces_for_layer(w_in, conv_pw_w, w_o, w1, w2, w_pg, w_ple):
    def kmaj(w):
        K, C = w.shape
        return np.ascontiguousarray(w.reshape(K // 128, 128, C).transpose(1, 0, 2)).reshape(128, -1)

    out = []
    for hp in range(3):
        cols = np.concatenate([np.arange(128) + o + hp * 128 for o in (0, 384, 768)])
        out.append(kmaj(w_in[:, cols]))
    out.append(kmaj(w_in[:, 1152:1664]))
    out.append(kmaj(conv_pw_w))
    d = np.arange(128)
    sw = (d // 64) * 64 + ((d % 64) + 32) % 64
    for hp in range(3):
        q = 1664 + hp * 128 + d
        qs = 1664 + hp * 128 + sw
        k = 2048 + hp * 128 + d
        ks = 2048 + hp * 128 + sw
        v = 2432 + hp * 128 + d
        g = 2816 + hp * 128 + d
        out.append(kmaj(w_in[:, np.concatenate([q, k, v, g])]))
    for half in range(2):
        out.append(kmaj(w_o[:, half * 512:(half + 1) * 512]))
    for g_ in range(8):
        out.append(kmaj(w1[:, g_ * 512:(g_ + 1) * 512]))
        out.append(kmaj(w2[g_ * 512:(g_ + 1) * 512, :]))
    for half in range(2):
        out.append(kmaj(w_ple[:, half * 512:(half + 1) * 512]))
        out.append(kmaj(w_pg[:, half * 512:(half + 1) * 512]))
    return out


def _consts():
    c = np.zeros((128, NCONST), np.float32)
    c[:, C_IDENT:C_IDENT + 128] = np.eye(128, dtype=np.float32)
    blk = np.zeros((128, 128), np.float32)
    blk[:64, :64] = 1.0
    blk[64:, 64:] = 1.0
    c[:, C_BLK:C_BLK + 128] = blk
    c[:, C_ONES:C_ONES + 128] = 1.0
    dsw = np.arange(128)
    sw_ = (dsw // 64) * 64 + ((dsw % 64) + 32) % 64
    perm = np.zeros((128, 128), np.float32)
    perm[sw_, dsw] = 1.0
    c[:, C_PERM:C_PERM + 128] = perm
    pos = np.arange(S, dtype=np.float32)
    inv_freq = (np.float32(10000.0) ** (-np.arange(0, 64, 2, dtype=np.float32) / np.float32(64))).astype(np.float32)
    ang = (pos[:, None] * inv_freq[None, :]).astype(np.float32)
    cos = np.cos(ang).astype(np.float32)
    sin = np.sin(ang).astype(np.float32)
    p = np.arange(128)
    dd = p % 64
    fi = dd % 32
    sign = np.where(dd < 32, -1.0, 1.0).astype(np.float32)
    c[:, C_ROPEC:C_ROPEC + S] = cos[:, fi].T
    c[:, C_ROPES:C_ROPES + S] = sin[:, fi].T * sign[:, None]
    lg = np.log(1.0 - 2.0 ** (-5.0 - np.arange(6, dtype=np.float64)))
    m = np.arange(128)[:, None]
    cc = np.arange(128)[None, :]
    for h in range(6):
        same = (m // 64) == (cc // 64)
        earlier = (m // 64) < (cc // 64)
        dm = np.where(same, np.exp(lg[h] * np.abs(cc - m)), np.where(earlier, np.exp(lg[h] * (cc - m)), 0.0)) / 8.0
        c[:, C_DM + h * 128:C_DM + (h + 1) * 128] = dm
        c[:, C_DECV + h] = np.exp(lg[h] * (127 - np.arange(128)))
    for hp in range(3):
        hh = 2 * hp + p // 64
        c[:, C_DQ + hp * 128:C_DQ + (hp + 1) * 128] = np.exp(lg[hh][:, None] * (np.arange(128)[None, :] + 1.0)) / 8.0
        c[:, C_DEC128 + hp] = np.exp(lg[hh] * 128.0)
    return c


def _bias_tables(rel_bias_l):
    pk = np.arange(128)[:, None]
    fq = np.arange(128)[None, :]
    out = np.empty((128, 3, 2, 5, 128), np.float32)
    for delta in range(5):
        rel = delta * 128 + fq - pk
        idx = np.clip(rel, -128, 128) + 128
        valid = np.ones((128, 128), bool)
        if delta == 4:
            valid = ~((pk < 64) & (fq >= 64))
        if delta == 0:
            valid = ~((pk >= 64) & (fq < 64))
        for h in range(6):
            tab = rel_bias_l[h][idx]
            out[:, h // 2, h % 2, delta, :] = np.where(valid, tab, np.float32(NEG))
    return out.reshape(128, -1)


def _vecs(l, I):
    v = np.zeros((128, NV), np.float32)
    p = np.arange(128)
    for j, name in enumerate(["norm_mix_g", "norm_ffn_g", "norm_ple_g"]):
        v[:, 8 * j:8 * j + 8] = I[name][l].reshape(8, 128).T
    v[:, 24] = I["qn_g"][l][p % 64]
    v[:, 25] = I["kn_g"][l][p % 64]
    v[:, 26:28] = I["conv_b"][l].reshape(2, 128).T
    v[:, 28:30] = I["conv_ln_g"][l].reshape(2, 128).T
    v[:, 30:32] = I["conv_ln_b"][l].reshape(2, 128).T
    v[:, 32:34] = I["conv_pw_b"][l].reshape(2, 128).T
    v[:, 34:37] = I["ret_gn_g"][l].reshape(3, 128).T
    cw = I["conv_w"][l]
    for c in range(2):
        v[:, 37 + c * 31:37 + (c + 1) * 31] = cw[:, c * 128:(c + 1) * 128].T
    return v


STATS = {}

class _Stop(Exception):
    pass


def build_nc(n_layers, piece_sizes, debug=None, stop=None):
    nc = bass.Bass("TRN2", target_bir_lowering=False)
    Sx = Sched(nc)
    TOT = sum(piece_sizes)
    xT_d = nc.dram_tensor("xT", [128, 8 * S], F32, kind="ExternalInput").ap()
    pT_d = nc.dram_tensor("pT", [128, n_layers * 2 * S], F32, kind="ExternalInput").ap()
    ws_d = nc.dram_tensor("wstream", [128, TOT], F32, kind="ExternalInput").ap()
    vec_d = nc.dram_tensor("vecs", [128, n_layers * NV], F32, kind="ExternalInput").ap()
    bt_d = nc.dram_tensor("btab", [128, n_layers * 3 * 1280], F32, kind="ExternalInput").ap()
    cst_d = nc.dram_tensor("consts", [128, NCONST], F32, kind="ExternalInput").ap()
    out_d = nc.dram_tensor("outT", [128, 8 * S], F32, kind="ExternalOutput").ap()
    dbg_d = {}

    hT = nc.alloc_sbuf_tensor("hT", [128, 8, S], F32)
    xnT = nc.alloc_sbuf_tensor("xnT", [128, 8, S], BF16)
    mixT = nc.alloc_sbuf_tensor("mixT", [128, 8, S], BF16)
    wring = nc.alloc_sbuf_tensor("wring", [128, NSLOT, SLOT], BF16)
    vecs = nc.alloc_sbuf_tensor("vecs_sb", [128, n_layers * NV], F32)
    cst = nc.alloc_sbuf_tensor("cst", [128, C_ROPEC], F32)
    cst2 = nc.alloc_sbuf_tensor("cst2", [128, NCONST - C_DM], F32)
    cbf = nc.alloc_sbuf_tensor("cbf", [128, 512], BF16)
    small = nc.alloc_sbuf_tensor("small", [128, 16], F32)
    tmpf = [nc.alloc_sbuf_tensor(f"tmpf{i}", [128, TT], F32) for i in range(5)]
    SCR_F32 = 8448
    scr = nc.alloc_sbuf_tensor("scr", [128, SCR_F32], F32)
    ps = [nc.alloc_psum_tensor(f"ps{i}", [128, TT], F32) for i in range(8)]

    ident_f = cst[:, C_IDENT:C_IDENT + 128]
    blk_f = cst[:, C_BLK:C_BLK + 128]
    ones_f = cst[:, C_ONES:C_ONES + 128]
    ident_b = cbf[:, 0:128]
    blk_b = cbf[:, 128:256]
    ones_b = cbf[:, 256:384]
    perm_b = cbf[:, 384:512]
    DM0 = 0
    DQ0 = C_DQ - C_DM
    DECV0 = C_DECV - C_DM
    DEC1280 = C_DEC128 - C_DM
    eps_t = small[:, 0:1]

    psn = [0]
    reserved = set()

    def psum():
        while True:
            b = psn[0] % 8
            psn[0] += 1
            if b not in reserved:
                return b

    tfn = [0]

    def tmp():
        i = tfn[0] % len(tmpf)
        tfn[0] += 1
        return tmpf[i], ("tmpf", i)

    class Carver:
        def __init__(self):
            self.off = 0
            self.keys = []

        def f32(self, name, n):
            ap = scr[:, self.off:self.off + n]
            self.off += n
            assert self.off <= SCR_F32, (name, self.off)
            self.keys.append(name)
            return ap

        def bf16(self, name, n):
            assert n % 2 == 0
            ap = scr[:, self.off:self.off + n // 2].bitcast(BF16)
            self.off += n // 2
            assert self.off <= SCR_F32, (name, self.off)
            self.keys.append(name)
            return ap

    offs = np.concatenate([[0], np.cumsum(piece_sizes)]).astype(int)
    NP = len(piece_sizes)
    wstate = {"issued": 0}

    def w_issue(i):
        slot = i % NSLOT
        n = piece_sizes[i]
        Sx.dma("pool", f"w{slot}", wring[:, slot, 0:n], ws_d[:, offs[i]:offs[i] + n], writes=[("w", slot)])

    def w_get(i):
        while wstate["issued"] <= i:
            w_issue(wstate["issued"])
            wstate["issued"] += 1
        return wring[:, i % NSLOT, :], ("w", i % NSLOT)

    def w_done(i):
        while wstate["issued"] < NP and wstate["issued"] < i + 1 + NSLOT:
            w_issue(wstate["issued"])
            wstate["issued"] += 1

    Sx.dma("sp", "cin0", cst[:, :], cst_d[:, 0:C_ROPEC], writes=["cst"])
    Sx.dma("sp", "cin1", cst2[:, :], cst_d[:, C_DM:NCONST], writes=["cst2"])
    Sx.dma("sp", "cin2", vecs[:, :], vec_d, writes=["vecs"])
    for k in range(8):
        Sx.dma("sp", f"xin{k}", hT[:, k, :], xT_d[:, k * S:(k + 1) * S], writes=[("h", k, t) for t in range(NT)])
    Sx.op("dve", lambda e: e.tensor_copy(cbf[:, :], cst[:, 0:512]), reads=["cst"], writes=["cbf"])
    Sx.op("dve", lambda e: e.memset(small[:, 0:1], EPS), writes=["small"])
    for i in range(min(NSLOT, NP)):
        w_get(i)

    def hkeys(k, t):
        return ("h", k, t)

    def rstd(src, srckey, scale, bufs=None):
        if bufs is not None:
            (ln_, lnk), (rs_, rsk_) = bufs
            Sx.op("act", lambda e: e.activation(ln_[:, :], src, AF.Ln, bias=eps_t, scale=scale),
                  reads=[srckey, "small"], writes=[lnk])
            Sx.op("act", lambda e: e.activation(rs_[:, :], ln_[:, :], AF.Exp, scale=-0.5), reads=[lnk], writes=[rsk_])
            return rs_, rsk_
        ln_, lnk = tmp()
        Sx.op("act", lambda e: e.activation(ln_[:, :], src, AF.Ln, bias=eps_t, scale=scale),
              reads=[srckey, "small"], writes=[lnk])
        rs_, rsk_ = tmp()
        Sx.op("act", lambda e: e.activation(rs_[:, :], ln_[:, :], AF.Exp, scale=-0.5), reads=[lnk], writes=[rsk_])
        return rs_, rsk_

    def rmsnorm(vb, gcol):
        for t in range(NT):
            sl = slice(t * TT, (t + 1) * TT)
            b = psum()
            for k in range(8):
                sq, sqk = tmp()
                sqb = sq[:, 0:TT // 2].bitcast(BF16)
                Sx.op("act", lambda e, k=k, sqb=sqb: e.activation(sqb, hT[:, k, sl], AF.Square),
                      reads=[hkeys(k, t)], writes=[sqk])
                Sx.op("pe", lambda e, k=k, sqb=sqb, b=b: e.matmul(ps[b][:, :], ones_b, sqb, start=(k == 0), stop=(k == 7)),
                      reads=[sqk, "cbf"], writes=[("ps", b)])
            rs, rsk = rstd(ps[b][:, :], ("ps", b), 1.0 / D)
            for k in range(8):
                Sx.op("dve", lambda e, k=k, rs=rs: e.scalar_tensor_tensor(
                    xnT[:, k, sl], hT[:, k, sl], vecs[:, vb + gcol + k:vb + gcol + k + 1], rs[:, :], ALU.mult, ALU.mult),
                    reads=[hkeys(k, t), rsk, "vecs"], writes=[("xn", k, t)])

    nst = {}

    def norm_p1(t):
        sl = slice(t * TT, (t + 1) * TT)
        sq8 = nst["sq8"]
        for k in range(8):
            Sx.op("act", lambda e, k=k: e.activation(sq8[:, k * TT:(k + 1) * TT], hT[:, k, sl], AF.Square),
                  reads=[hkeys(k, t)], writes=[("sq8", k)])

    def norm_p2(t, vb, gcol):
        sl = slice(t * TT, (t + 1) * TT)
        sq8 = nst["sq8"]
        b = psum()
        for k in range(8):
            Sx.op("pe", lambda e, k=k, b=b: e.matmul(ps[b][:, :], ones_b, sq8[:, k * TT:(k + 1) * TT], start=(k == 0), stop=(k == 7)),
                  reads=[("sq8", k), "cbf"], writes=[("ps", b)])
        rs, rsk = rstd(ps[b][:, :], ("ps", b), 1.0 / D)
        for k in range(8):
            Sx.op("dve", lambda e, k=k, rs=rs: e.scalar_tensor_tensor(
                xnT[:, k, sl], hT[:, k, sl], vecs[:, vb + gcol + k:vb + gcol + k + 1], rs[:, :], ALU.mult, ALU.mult),
                reads=[hkeys(k, t), rsk, "vecs"], writes=[("xn", k, t)])

    def project(wap, wkey, col0, nchunks, src, srckey, t, nk=8, kstride=None):
        sl = slice(t * TT, (t + 1) * TT)
        banks = []
        for c in range(nchunks):
            b = psum()
            for k in range(nk):
                lhsT = wap[:, k * kstride + col0 + c * 128: k * kstride + col0 + (c + 1) * 128]
                Sx.op("pe", lambda e, b=b, lhsT=lhsT, k=k: e.matmul(ps[b][:, :], lhsT, src[:, k, sl], start=(k == 0), stop=(k == nk - 1)),
                      reads=[wkey, (srckey, k, t)], writes=[("ps", b)])
            banks.append(b)
        return banks

    dbgst = nc.alloc_sbuf_tensor("dbgst", [128, TT], F32) if debug else None

    def dump(name, ap, key, shape=None):
        if not debug or name not in debug:
            return
        d = nc.dram_tensor("dbg_" + name, [128, TT], F32, kind="ExternalOutput").ap()
        dbg_d[name] = d
        Sx.op("dve", lambda e: e.tensor_copy(dbgst[:, :], ap), reads=key, writes=["dbgst"])
        Sx.dma("sp", "dbg", d, dbgst[:, :], reads=["dbgst"], writes=["dbgo_" + name])

    pi = [0]

    def chk(name):
        if stop == name:
            raise _Stop()

    def layer_body(l):
        vb = l * NV
        if l == 0:
            rmsnorm(vb, 0)
        if l == 0:
            dump("xn", xnT[:, 3, 512:1024], [("xn", 3, 1)])
        chk('norm')
        Sx.op("dve", lambda e: e.tensor_scalar(small[:, 1:2], vecs[:, vb + 24:vb + 25], 0.125, None, ALU.mult),
              reads=["vecs"], writes=["gq8"])
        gq8 = small[:, 1:2]
        gk = vecs[:, vb + 25:vb + 26]

        for hp in range(3):
            cv = Carver()
            kT = cv.bf16("a_kT", S)
            vaug = cv.bf16("a_vaug", 16 * 2 * 66)
            qT = cv.bf16("a_qT", S)
            PTr = cv.bf16("a_PTr", 7 * 2 * 640)
            atok = [cv.bf16(f"a_atok{i}", 128) for i in range(2)]
            btb = cv.bf16("a_bt", 1280)
            rec = cv.f32("a_rec", 4)
            cv.keys += [("a_kT", t_) for t_ in range(NT)] + [("a_vaug", t_) for t_ in range(NT)]
            cv.keys += [("a_qT", t_) for t_ in range(NT)] + [("a_PT", s_, h_) for s_ in range(7) for h_ in range(2)]
            Sx.phase(cv.keys)
            vaug4 = vaug.rearrange("p (j h d) -> p j h d", j=16, h=2)
            Sx.dma("pool", "bt", btb, bt_d[:, (l * 3 + hp) * 1280:(l * 3 + hp + 1) * 1280], writes=["a_bt"])
            Sx.op("act", lambda e: e.activation(btb, btb, AF.Exp), reads=["a_bt"], writes=["a_bt"])
            Sx.op("dve", lambda e: e.memset(vaug4[:, :, :, 64:65], 1.0), writes=[("a_vaug", t_) for t_ in range(NT)])
            wap, wkey = w_get(pi[0])
            for t in range(NT):
                sl = slice(t * TT, (t + 1) * TT)
                sqbs = []
                (bq,) = project(wap, wkey, 0, 1, xnT, "xn", t, kstride=384)
                for b in (bq,):
                    sq, sqk = tmp()
                    sqb = sq[:, 0:TT // 2].bitcast(BF16)
                    Sx.op("act", lambda e, b=b, sqb=sqb: e.activation(sqb, ps[b][:, :], AF.Square),
                          reads=[("ps", b)], writes=[sqk])
                    sqbs.append((sqb, sqk))
                (bk,) = project(wap, wkey, 128, 1, xnT, "xn", t, kstride=384)
                for b in (bk,):
                    sq, sqk = tmp()
                    sqb = sq[:, 0:TT // 2].bitcast(BF16)
                    Sx.op("act", lambda e, b=b, sqb=sqb: e.activation(sqb, ps[b][:, :], AF.Square),
                          reads=[("ps", b)], writes=[sqk])
                    sqbs.append((sqb, sqk))
                (bv,) = project(wap, wkey, 256, 1, xnT, "xn", t, kstride=384)
                vt_, vtk = tmp()
                vT = vt_[:, 0:TT // 2].bitcast(BF16)
                Sx.op("act", lambda e, vT=vT: e.copy(vT, ps[bv][:, :]), reads=[("ps", bv)], writes=[vtk])
                b2s = []
                for (sqb, sqk) in sqbs:
                    b2 = psum()
                    Sx.op("pe", lambda e, b2=b2, sqb=sqb: e.matmul(ps[b2][:, :], blk_b, sqb, start=True, stop=True),
                          reads=[sqk, "cbf"], writes=[("ps", b2)])
                    b2s.append(b2)
                bt_ = psum()
                psb = ps[bt_][:, :].bitcast(BF16)
                for i in range(4):
                    Sx.op("pe", lambda e, i=i, psb=psb, vT=vT: e.transpose(psb[:, i * 128:(i + 1) * 128], vT[:, i * 128:(i + 1) * 128], ident_b),
                          reads=[vtk, "cbf"], writes=[("ps", bt_)])
                for b, b2, dst, dkey, gain in ((bq, b2s[0], qT[:, sl], ("a_qT", t), gq8), (bk, b2s[1], kT[:, sl], ("a_kT", t), gk)):
                    rs, rsk = rstd(ps[b2][:, :], ("ps", b2), 1.0 / 64)
                    Sx.op("dve", lambda e, b=b, dst=dst, gain=gain, rs=rs: e.scalar_tensor_tensor(
                        dst, ps[b][:, :], gain, rs[:, :], ALU.mult, ALU.mult),
                        reads=[("ps", b), rsk, "vecs", "gq8"], writes=[dkey])
                Sx.op("dve", lambda e, psb=psb: e.tensor_copy(
                    vaug4[:, 4 * t:4 * t + 4, :, 0:64],
                    psb[:, 0:512].rearrange("p (j h d) -> p j h d", j=4, h=2)),
                    reads=[("ps", bt_)], writes=[("a_vaug", t)])
            w_done(pi[0])
            pi[0] += 1

            def qk(kt):
                nq = min(5, 16 - kt)
                ncol = nq * 128
                n1 = min(512, ncol)
                slot = kt % 7
                for h in range(2):
                    hb = h * 64
                    PTs = PTr[:, (slot * 2 + h) * 640:(slot * 2 + h + 1) * 640]
                    pk = ("a_PT", slot, h)
                    qkeys = [("a_qT", t_) for t_ in range(kt // 4, min(NT - 1, (kt * 128 + ncol - 1) // TT) + 1)]
                    bA = psum()
                    Sx.op("pe", lambda e, bA=bA, hb=hb: e.matmul(
                        ps[bA][:, 0:n1], kT[hb:hb + 64, kt * 128:(kt + 1) * 128], qT[hb:hb + 64, kt * 128:kt * 128 + n1],
                        start=True, stop=True),
                        reads=[("a_kT", kt // 4)] + qkeys, writes=[("ps", bA)])
                    Sx.op("act", lambda e, bA=bA, PTs=PTs: e.activation(PTs[:, 0:n1], ps[bA][:, 0:n1], AF.Exp),
                          reads=[("ps", bA)], writes=[pk])
                    if ncol > 512:
                        bB = psum()
                        Sx.op("pe", lambda e, bB=bB, hb=hb: e.matmul(
                            ps[bB][:, 0:128], kT[hb:hb + 64, kt * 128:(kt + 1) * 128], qT[hb:hb + 64, kt * 128 + 512:kt * 128 + 640],
                            start=True, stop=True),
                            reads=[("a_kT", kt // 4)] + qkeys, writes=[("ps", bB)])
                        Sx.op("act", lambda e, bB=bB, PTs=PTs: e.activation(PTs[:, 512:640], ps[bB][:, 0:128], AF.Exp),
                              reads=[("ps", bB)], writes=[pk])

            def qk_mask(kt):
                ncol = min(5, 16 - kt) * 128
                slot = kt % 7
                for h in range(2):
                    PTs = PTr[:, (slot * 2 + h) * 640:(slot * 2 + h + 1) * 640]
                    pk = ("a_PT", slot, h)
                    Sx.op("dve", lambda e, PTs=PTs, h=h: e.tensor_tensor(
                        PTs[:, 0:ncol], PTs[:, 0:ncol], btb[:, h * 640:h * 640 + ncol], ALU.mult),
                        reads=[pk, "a_bt"], writes=[pk])

            st_ = {}

            def pv(j):
                t = j // 4
                jj = j % 4
                if jj == 0:
                    st_[t] = psum()
                    reserved.add(st_[t])
                kts = list(range(max(0, j - 4), j + 1))
                bo = psum()
                for h in range(2):
                    for ki, kt in enumerate(kts):
                        delta = j - kt
                        base = ((kt % 7) * 2 + h) * 640 + delta * 128
                        Sx.op("pe", lambda e, h=h, kt=kt, base=base, ki=ki: e.matmul(
                            ps[bo][:, h * 128:h * 128 + 65], PTr[:, base:base + 128],
                            vaug4[:, kt, h, 0:65], start=(ki == 0), stop=(ki == len(kts) - 1)),
                            reads=[("a_PT", kt % 7, h), ("a_vaug", kt // 4)], writes=[("ps", bo)])
                Sx.op("dve", lambda e: e.reciprocal(
                    rec[:, 0:2], ps[bo][:, :].rearrange("p (h d) -> p h d", h=4)[:, 0:2, 64]),
                    reads=[("ps", bo)], writes=["a_rec"])
                at = atok[j % 2]
                atk = f"a_atok{j % 2}"
                for h in range(2):
                    Sx.op("dve", lambda e, h=h: e.tensor_scalar(
                        at[:, h * 64:(h + 1) * 64], ps[bo][:, h * 128:h * 128 + 64], rec[:, h:h + 1], None, ALU.mult),
                        reads=[("ps", bo), "a_rec"], writes=[atk])

            def pv_tr(j):
                t = j // 4
                jj = j % 4
                bo_t = st_[t]
                psbo = ps[bo_t][:, :].bitcast(BF16)
                at = atok[j % 2]
                atk = f"a_atok{j % 2}"
                Sx.op("pe", lambda e: e.transpose(psbo[:, jj * 128:(jj + 1) * 128], at, ident_b),
                      reads=[atk, "cbf"], writes=[("ps", bo_t)])
                if jj == 3:
                    Sx.op("act", lambda e: e.copy(mixT[:, hp, t * TT:(t + 1) * TT], psbo[:, 0:512]),
                          reads=[("ps", bo_t)], writes=[("mix", hp, t)])
                    reserved.discard(bo_t)

            qk(0)
            qk_mask(0)
            qk(1)
            qk_mask(1)
            for kt in range(16):
                if kt + 2 < 16:
                    qk(kt + 2)
                pv(kt)
                if kt >= 1:
                    pv_tr(kt - 1)
                if kt + 2 < 16:
                    qk_mask(kt + 2)
            pv_tr(15)
        if l == 0:
            dump("att0", mixT[:, 0, 0:512], [("mix", 0, 0)])
            dump("att1", mixT[:, 1, 1024:1536], [("mix", 1, 2)])

        chk('att')
        cv = Carver()
        glu = cv.bf16("c_glu", 2 * (S + 32))
        diag = cv.bf16("c_diag", 2 * CONVK * 128)
        y32 = cv.f32("c_y32", 2 * TT)
        sbf = cv.bf16("c_s", 2 * TT)
        cv.keys += [("c_diag", c_, r_) for c_ in range(2) for r_ in range(3)]
        Sx.phase(cv.keys)
        glu3 = glu.rearrange("p (c n) -> p c n", c=2)
        Sx.op("dve", lambda e: e.memset(glu3[:, :, 0:30], 0.0), writes=["c_glu"])
        DG = [(0, 11), (11, 21), (21, 31)]
        for c in range(2):
            for gi, (j0, j1) in enumerate(DG):
                nj = j1 - j0
                dst_ = diag[:, (c * CONVK + j0) * 128:(c * CONVK + j1) * 128].rearrange("p (j q) -> p j q", j=nj)
                wv = vecs[:, vb + 37 + c * CONVK + j0:vb + 37 + c * CONVK + j1]
                Sx.op("dve", lambda e, dst_=dst_, wv=wv, nj=nj: e.tensor_tensor(
                    dst_, ident_f.unsqueeze(1).broadcast_to([128, nj, 128]),
                    wv.unsqueeze(2).broadcast_to([128, nj, 128]), ALU.mult),
                    reads=["cst", "vecs"], writes=[("c_diag", c, gi)])
        wap, wkey = w_get(pi[0])
        wpw, wpwkey = w_get(pi[0] + 1)
        cst_ = {}

        def conv_A(t):
            sl = slice(t * TT, (t + 1) * TT)
            ba0, ba1, bg0, bg1 = project(wap, wkey, 0, 4, xnT, "xn", t, kstride=512)
            cst_[("A", t)] = (ba0, ba1, bg0, bg1)

        def conv_GLU(t):
            ba0, ba1, bg0, bg1 = cst_[("A", t)]
            for c, (ba, bg) in enumerate(((ba0, bg0), (ba1, bg1))):
                sg, sgk = tmp()
                Sx.op("act", lambda e, bg=bg, sg=sg: e.activation(sg[:, :], ps[bg][:, :], AF.Sigmoid),
                      reads=[("ps", bg)], writes=[sgk])
                Sx.op("dve", lambda e, c=c, ba=ba, sg=sg: e.tensor_tensor(
                    glu3[:, c, 30 + t * TT:30 + (t + 1) * TT], ps[ba][:, :], sg[:, :], ALU.mult),
                    reads=[("ps", ba), sgk], writes=["c_glu"])

        def conv_B(t):
            by = []
            for c in range(2):
                b = psum()
                for jt in range(CONVK):
                    Sx.op("pe", lambda e, c=c, jt=jt, b=b: e.matmul(
                        ps[b][:, :], diag[:, (c * CONVK + jt) * 128:(c * CONVK + jt + 1) * 128],
                        glu3[:, c, t * TT + jt:t * TT + jt + TT], start=(jt == 0), stop=(jt == CONVK - 1)),
                        reads=[("c_diag", c, 0 if jt < 11 else (1 if jt < 21 else 2)), "c_glu"], writes=[("ps", b)])
                by.append(b)
            sqs = []
            for c in range(2):
                Sx.op("act", lambda e, c=c: e.activation(
                    y32[:, c * TT:(c + 1) * TT], ps[by[c]][:, :], AF.Identity, bias=vecs[:, vb + 26 + c:vb + 27 + c], scale=1.0),
                    reads=[("ps", by[c]), "vecs"], writes=["c_y32"])
                sq_, sqk_ = tmp()
                sqs.append((sq_, sqk_))
                Sx.op("act", lambda e, c=c, sq_=sq_: e.activation(
                    sq_[:, :], y32[:, c * TT:(c + 1) * TT], AF.Square),
                    reads=["c_y32"], writes=[sqk_])
            b1 = psum()
            b2 = psum()
            for c in range(2):
                Sx.op("pe", lambda e, c=c: e.matmul(ps[b1][:, :], ones_f, y32[:, c * TT:(c + 1) * TT], start=(c == 0), stop=(c == 1)),
                      reads=["c_y32", "cst"], writes=[("ps", b1)])
            for c in range(2):
                Sx.op("pe", lambda e, c=c: e.matmul(ps[b2][:, :], ones_f, sqs[c][0][:, :], start=(c == 0), stop=(c == 1)),
                      reads=[sqs[c][1], "cst"], writes=[("ps", b2)])
            cst_[("B", t)] = (b1, b2)

        def conv_CH(t):
            b1, b2 = cst_[("B", t)]
            mean, mk = tmp()
            Sx.op("dve", lambda e, mean=mean: e.tensor_scalar(mean[:, :], ps[b1][:, :], 1.0 / 256, None, ALU.mult),
                  reads=[("ps", b1)], writes=[mk])
            msq, msk = tmp()
            Sx.op("dve", lambda e, mean=mean, msq=msq: e.tensor_tensor(msq[:, :], mean[:, :], mean[:, :], ALU.mult),
                  reads=[mk], writes=[msk])
            var, vk = tmp()
            Sx.op("dve", lambda e, var=var, msq=msq: e.scalar_tensor_tensor(
                var[:, :], ps[b2][:, :], 1.0 / 256, msq[:, :], ALU.mult, ALU.subtract),
                reads=[("ps", b2), msk], writes=[vk])
            rstd(var[:, :], vk, 1.0, bufs=((msq, msk), (var, vk)))
            for c in range(2):
                ysl = y32[:, c * TT:(c + 1) * TT]
                Sx.op("dve", lambda e, ysl=ysl, mean=mean: e.tensor_tensor(ysl, ysl, mean[:, :], ALU.subtract),
                      reads=["c_y32", mk], writes=["c_y32"])
                Sx.op("dve", lambda e, ysl=ysl, var=var: e.tensor_tensor(ysl, ysl, var[:, :], ALU.mult),
                      reads=["c_y32", vk], writes=["c_y32"])
                Sx.op("act", lambda e, ysl=ysl, c=c: e.activation(
                    sbf[:, c * TT:(c + 1) * TT], ysl, AF.Silu, bias=vecs[:, vb + 30 + c:vb + 31 + c],
                    scale=vecs[:, vb + 28 + c:vb + 29 + c]),
                    reads=["c_y32", "vecs"], writes=["c_s"])

        def conv_PW(t):
            sl = slice(t * TT, (t + 1) * TT)
            for co in range(2):
                b = psum()
                for ci in range(2):
                    Sx.op("pe", lambda e, b=b, ci=ci, co=co: e.matmul(
                        ps[b][:, :], wpw[:, ci * 256 + co * 128:ci * 256 + (co + 1) * 128], sbf[:, ci * TT:(ci + 1) * TT],
                        start=(ci == 0), stop=(ci == 1)),
                        reads=[wpwkey, "c_s"], writes=[("ps", b)])
                Sx.op("act", lambda e, b=b, co=co: e.activation(
                    mixT[:, 3 + co, sl], ps[b][:, :], AF.Identity, bias=vecs[:, vb + 32 + co:vb + 33 + co], scale=1.0),
                    reads=[("ps", b), "vecs"], writes=[("mix", 3 + co, t)])

        conv_A(0)
        conv_GLU(0)
        for t in range(NT):
            conv_B(t)
            if t + 1 < NT:
                conv_A(t + 1)
            conv_CH(t)
            if t + 1 < NT:
                conv_GLU(t + 1)
            conv_PW(t)
        w_done(pi[0])
        w_done(pi[0] + 1)
        pi[0] += 2
        if l == 0:
            dump("conv0", mixT[:, 3, 0:512], [("mix", 3, 0)])
            dump("conv1", mixT[:, 4, 512:1024], [("mix", 4, 1)])

        chk('conv')
        for hp in range(3):
            cv = Carver()
            qr = cv.bf16("r_qr", TT)
            kr = cv.bf16("r_kr", TT)
            qd = cv.bf16("r_qd", TT)
            vT = cv.bf16("r_vT", TT)
            gT = cv.bf16("r_gT", TT)
            ktok = cv.bf16("r_ktok", TT)
            vtok = cv.bf16("r_vtok", TT)
            vdec = cv.bf16("r_vdec", TT)
            AT = cv.bf16("r_AT", 2 * TT)
            st32 = [cv.f32(f"r_st{i}", 64) for i in range(2)]
            stbf = cv.bf16("r_stbf", 4 * 64)
            ropeC = [cv.f32(f"r_rc{i}", TT) for i in range(2)]
            ropeS = [cv.f32(f"r_rs{i}", TT) for i in range(2)]
            cv.keys += [("r_AT", i_, h_) for i_ in range(4) for h_ in range(2)] + [("r_stbf", i_) for i_ in range(4)]
            Sx.phase(cv.keys)
            wA, wAk = w_get(pi[0])
            Sx.op("dve", lambda e: e.memset(st32[0], 0.0), writes=["r_st0"])
            cur = 0
            dq = cst2[:, DQ0 + hp * 128:DQ0 + (hp + 1) * 128]
            for t in range(NT):
                sl = slice(t * TT, (t + 1) * TT)
                rc, rs_ = ropeC[t % 2], ropeS[t % 2]
                rck, rsk_ = f"r_rc{t % 2}", f"r_rs{t % 2}"
                Sx.dma("sp", f"ropec{t % 2}", rc, cst_d[:, C_ROPEC + t * TT:C_ROPEC + (t + 1) * TT], writes=[rck])
                Sx.dma("sp", f"ropes{t % 2}", rs_, cst_d[:, C_ROPES + t * TT:C_ROPES + (t + 1) * TT], writes=[rsk_])
                part = {}
                for nm, col in (("q", 0), ("k", 128)):
                    (b_,) = project(wA, wAk, col, 1, xnT, "xn", t, kstride=512)
                    xb_, xbk = tmp()
                    xb = xb_[:, 0:TT // 2].bitcast(BF16)
                    Sx.op("act", lambda e, b_=b_, xb=xb: e.copy(xb, ps[b_][:, :]), reads=[("ps", b_)], writes=[xbk])
                    t1, t1k = tmp()
                    Sx.op("dve", lambda e, b_=b_, t1=t1: e.tensor_tensor(t1[:, :], ps[b_][:, :], rc, ALU.mult),
                          reads=[("ps", b_), rck, xbk], writes=[t1k])
                    part[nm] = (xb, xbk, t1, t1k)
                (bv,) = project(wA, wAk, 256, 1, xnT, "xn", t, kstride=512)
                Sx.op("act", lambda e: e.copy(vT, ps[bv][:, :]), reads=[("ps", bv)], writes=["r_vT"])
                for nm, dst, dk in (("q", qr, "r_qr"), ("k", kr, "r_kr")):
                    xb, xbk, t1, t1k = part[nm]
                    bs_ = psum()
                    Sx.op("pe", lambda e, bs_=bs_, xb=xb: e.matmul(ps[bs_][:, :], perm_b, xb, start=True, stop=True),
                          reads=[xbk, "cbf"], writes=[("ps", bs_)])
                    t2, t2k = tmp()
                    Sx.op("dve", lambda e, bs_=bs_, t2=t2: e.tensor_tensor(t2[:, :], ps[bs_][:, :], rs_, ALU.mult),
                          reads=[("ps", bs_), rsk_], writes=[t2k])
                    Sx.op("dve", lambda e, t1=t1, t2=t2, dst=dst: e.tensor_tensor(dst, t1[:, :], t2[:, :], ALU.add),
                          reads=[t1k, t2k], writes=[dk])
                (bg,) = project(wA, wAk, 384, 1, xnT, "xn", t, kstride=512)
                Sx.op("act", lambda e: e.activation(gT, ps[bg][:, :], AF.Silu), reads=[("ps", bg)], writes=["r_gT"])
                chk('r_proj')
                btv = psum()
                psv = ps[btv][:, :].bitcast(BF16)
                for i in range(4):
                    Sx.op("pe", lambda e, i=i: e.transpose(psv[:, i * 128:(i + 1) * 128], vT[:, i * 128:(i + 1) * 128], ident_b),
                          reads=["r_vT", "cbf"], writes=[("ps", btv)])
                bss = []
                for h in range(2):
                    bs = psum()
                    bss.append(bs)
                    for i in range(4):
                        Sx.op("pe", lambda e, i=i, h=h, bs=bs: e.matmul(
                            ps[bs][:, i * 128:(i + 1) * 128],
                            kr[h * 64:(h + 1) * 64, i * 128:(i + 1) * 128],
                            qr[h * 64:(h + 1) * 64, i * 128:(i + 1) * 128], start=True, stop=True),
                            reads=["r_kr", "r_qr"], writes=[("ps", bs)])
                btk = psum()
                psk = ps[btk][:, :].bitcast(BF16)
                for i in range(4):
                    Sx.op("pe", lambda e, i=i: e.transpose(psk[:, i * 128:(i + 1) * 128], kr[:, i * 128:(i + 1) * 128], ident_b),
                          reads=["r_kr", "cbf"], writes=[("ps", btk)])
                for h in range(2):
                    Sx.op("dve", lambda e, h=h: e.tensor_scalar(
                        vdec.rearrange("p (i c) -> p i c", i=4)[:, :, h * 64:(h + 1) * 64],
                        psv[:, 0:512].rearrange("p (i c) -> p i c", i=4)[:, :, h * 64:(h + 1) * 64],
                        cst2[:, DECV0 + 2 * hp + h:DECV0 + 2 * hp + h + 1], None, ALU.mult),
                        reads=[("ps", btv), "cst2"], writes=["r_vdec"])
                Sx.op("dve", lambda e: e.tensor_copy(vtok, psv[:, 0:512]), reads=[("ps", btv)], writes=["r_vtok"])
                Sx.op("act", lambda e: e.copy(ktok, psk[:, 0:512]), reads=[("ps", btk)], writes=["r_ktok"])
                Sx.op("act", lambda e, cur=cur: e.copy(stbf[:, 0:64], st32[cur]),
                      reads=[f"r_st{cur}"], writes=[("r_stbf", 0)])
                for i in range(4):
                    for h in range(2):
                        dmh = cst2[:, DM0 + (2 * hp + h) * 128:DM0 + (2 * hp + h + 1) * 128]
                        Sx.op("dve", lambda e, i=i, h=h, dmh=dmh: e.tensor_tensor(
                            AT[:, (i * 2 + h) * 128:(i * 2 + h + 1) * 128],
                            ps[bss[h]][:, i * 128:(i + 1) * 128], dmh, ALU.mult),
                            reads=[("ps", bss[h]), "cst2"], writes=[("r_AT", i, h)])
                Sx.op("dve", lambda e: e.tensor_tensor(
                    qd.rearrange("p (i c) -> p i c", i=4), qr.rearrange("p (i c) -> p i c", i=4),
                    dq.unsqueeze(1).broadcast_to([128, 4, 128]), ALU.mult),
                    reads=["r_qr", "cst2"], writes=["r_qd"])
                chk('r_tr')
                bkv = psum()
                for i in range(4):
                    for h in range(2):
                        Sx.op("pe", lambda e, i=i, h=h: e.matmul(
                            ps[bkv][h * 64:(h + 1) * 64, i * 64:(i + 1) * 64],
                            ktok[:, i * 128 + h * 64:i * 128 + (h + 1) * 64],
                            vdec[:, i * 128 + h * 64:i * 128 + (h + 1) * 64], start=True, stop=True),
                            reads=["r_ktok", "r_vdec"], writes=[("ps", bkv)])
                for i in range(4):
                    if i > 0:
                        Sx.op("act", lambda e, i=i, cur=cur: e.copy(stbf[:, i * 64:(i + 1) * 64], st32[cur]),
                              reads=[f"r_st{cur}"], writes=[("r_stbf", i)])
                    Sx.op("dve", lambda e, i=i, cur=cur: e.scalar_tensor_tensor(
                        st32[1 - cur], st32[cur], cst2[:, DEC1280 + hp:DEC1280 + hp + 1], ps[bkv][:, i * 64:(i + 1) * 64],
                        ALU.mult, ALU.add),
                        reads=[f"r_st{cur}", ("ps", bkv), "cst2"], writes=[f"r_st{1 - cur}"])
                    cur = 1 - cur
                chk('r_kv')
                chk('r_s')
                bo = psum()
                for i in range(4):
                    T = 4 * t + i
                    for h in range(2):
                        o = ps[bo][h * 64:(h + 1) * 64, i * 128:(i + 1) * 128]
                        Sx.op("pe", lambda e, o=o, i=i, h=h, T=T: e.matmul(
                            o, vtok[:, i * 128 + h * 64:i * 128 + (h + 1) * 64],
                            AT[:, (i * 2 + h) * 128:(i * 2 + h + 1) * 128], start=True, stop=(T == 0)),
                            reads=["r_vtok", ("r_AT", i, h)], writes=[("ps", bo)])
                        if T > 0:
                            Sx.op("pe", lambda e, o=o, i=i, h=h: e.matmul(
                                o, stbf[h * 64:(h + 1) * 64, i * 64:(i + 1) * 64],
                                qd[h * 64:(h + 1) * 64, i * 128:(i + 1) * 128], start=False, stop=True),
                                reads=[("r_stbf", i), "r_qd"], writes=[("ps", bo)])
                chk('r_o')
                o32, ok = tmp()
                s32, sk_ = tmp()
                Sx.op("act", lambda e, o32=o32: e.copy(o32[:, :], ps[bo][:, :]), reads=[("ps", bo)], writes=[ok])
                Sx.op("act", lambda e, s32=s32: e.activation(s32[:, :], ps[bo][:, :], AF.Square), reads=[("ps", bo)], writes=[sk_])
                b1 = psum()
                b2 = psum()
                Sx.op("pe", lambda e, o32=o32: e.matmul(ps[b1][:, :], blk_f, o32[:, :], start=True, stop=True),
                      reads=[ok, "cst"], writes=[("ps", b1)])
                Sx.op("pe", lambda e, s32=s32: e.matmul(ps[b2][:, :], blk_f, s32[:, :], start=True, stop=True),
                      reads=[sk_, "cst"], writes=[("ps", b2)])
                mean, mk = tmp()
                Sx.op("dve", lambda e, mean=mean: e.tensor_scalar(mean[:, :], ps[b1][:, :], 1.0 / 64, None, ALU.mult),
                      reads=[("ps", b1)], writes=[mk])
                Sx.op("dve", lambda e, mean=mean, s32=s32: e.tensor_tensor(s32[:, :], mean[:, :], mean[:, :], ALU.mult),
                      reads=[mk], writes=[sk_])
                var, vk = tmp()
                Sx.op("dve", lambda e, var=var, s32=s32: e.scalar_tensor_tensor(
                    var[:, :], ps[b2][:, :], 1.0 / 64, s32[:, :], ALU.mult, ALU.subtract),
                    reads=[("ps", b2), sk_], writes=[vk])
                rstd(var[:, :], vk, 1.0, bufs=((s32, sk_), (var, vk)))
                Sx.op("dve", lambda e, o32=o32, mean=mean: e.tensor_tensor(o32[:, :], o32[:, :], mean[:, :], ALU.subtract),
                      reads=[ok, mk], writes=[ok])
                Sx.op("dve", lambda e, o32=o32, var=var: e.tensor_tensor(o32[:, :], o32[:, :], var[:, :], ALU.mult),
                      reads=[ok, vk], writes=[ok])
                Sx.op("dve", lambda e, o32=o32: e.scalar_tensor_tensor(
                    mixT[:, 5 + hp, sl], o32[:, :], vecs[:, vb + 34 + hp:vb + 35 + hp], gT, ALU.mult, ALU.mult),
                    reads=[ok, "vecs", "r_gT"], writes=[("mix", 5 + hp, t)])
            w_done(pi[0])
            pi[0] += 1
        if l == 0:
            dump("ret0", mixT[:, 5, 0:512], [("mix", 5, 0)])
            dump("ret1", mixT[:, 7, 512:1024], [("mix", 7, 1)])

        chk('ret')
        cv = Carver()
        hid = [cv.bf16(f"f_hid{i}", 4 * TT) for i in range(2)]
        pTs = cv.bf16("f_pT", 2 * S)
        nst["sq8"] = cv.bf16("f_sq8", 8 * TT)
        cv.keys += [("sq8", k_) for k_ in range(8)]
        Sx.phase(cv.keys)
        Sx.dma("pool", "pT", pTs, pT_d[:, l * 2 * S:(l + 1) * 2 * S], writes=["f_pT"])

        def wo_tile(wap, wkey, half, m, t):
            mo = half * 4 + m
            sl = slice(t * TT, (t + 1) * TT)
            b = psum()
            for k in range(8):
                Sx.op("pe", lambda e, b=b, k=k, m=m: e.matmul(
                    ps[b][:, :], wap[:, k * 512 + m * 128:k * 512 + (m + 1) * 128], mixT[:, k, sl],
                    start=(k == 0), stop=(k == 7)),
                    reads=[wkey, ("mix", k, t)], writes=[("ps", b)])
            Sx.op("dve", lambda e, b=b, mo=mo: e.tensor_tensor(hT[:, mo, sl], hT[:, mo, sl], ps[b][:, :], ALU.add),
                  reads=[("ps", b), hkeys(mo, t)], writes=[hkeys(mo, t)])

        wap, wkey = w_get(pi[0])
        for m in range(4):
            for t in range(NT):
                wo_tile(wap, wkey, 0, m, t)
        w_done(pi[0])
        pi[0] += 1
        wap, wkey = w_get(pi[0])
        for t in range(NT):
            for m in range(4):
                wo_tile(wap, wkey, 1, m, t)
            if t >= 1:
                norm_p2(t - 1, vb, 8)
            norm_p1(t)
        norm_p2(NT - 1, vb, 8)
        w_done(pi[0])
        pi[0] += 1
        if l == 0:
            dump("h1", hT[:, 2, 512:1024], [("h", 2, 1)])

        chk('wo')
        steps = [(g_, t) for g_ in range(8) for t in range(NT)]
        pbase = pi[0]

        def ffn_up(n):
            g_, t = steps[n]
            w1a, w1k = w_get(pbase + 2 * g_)
            hd = hid[n % 2]
            hdk = f"f_hid{n % 2}"
            banks = project(w1a, w1k, 0, 4, xnT, "xn", t, kstride=512)
            for c, b in enumerate(banks):
                r, rk = tmp()
                Sx.op("act", lambda e, b=b, r=r: e.activation(r[:, :], ps[b][:, :], AF.Relu), reads=[("ps", b)], writes=[rk])
                Sx.op("dve", lambda e, r=r, c=c, hd=hd: e.tensor_tensor(hd[:, c * TT:(c + 1) * TT], r[:, :], r[:, :], ALU.mult),
                      reads=[rk], writes=[hdk])
            if t == NT - 1:
                w_done(pbase + 2 * g_)

        def ffn_down(n):
            g_, t = steps[n]
            sl = slice(t * TT, (t + 1) * TT)
            w2a, w2k = w_get(pbase + 2 * g_ + 1)
            hd = hid[n % 2]
            hdk = f"f_hid{n % 2}"
            for m in range(8):
                b = psum()
                for c in range(4):
                    Sx.op("pe", lambda e, b=b, c=c, m=m, hd=hd: e.matmul(
                        ps[b][:, :], w2a[:, c * 1024 + m * 128:c * 1024 + (m + 1) * 128], hd[:, c * TT:(c + 1) * TT],
                        start=(c == 0), stop=(c == 3)),
                        reads=[w2k, hdk], writes=[("ps", b)])
                Sx.op("dve", lambda e, b=b, m=m: e.tensor_tensor(hT[:, m, sl], hT[:, m, sl], ps[b][:, :], ALU.add),
                      reads=[("ps", b), hkeys(m, t)], writes=[hkeys(m, t)])
            if t == NT - 1:
                w_done(pbase + 2 * g_ + 1)

        ffn_up(0)
        for n in range(len(steps)):
            if n + 1 < len(steps):
                ffn_up(n + 1)
            ffn_down(n)
            g_, t = steps[n]
            if g_ == 7:
                if t >= 1:
                    norm_p2(t - 1, vb, 16)
                norm_p1(t)
        norm_p2(NT - 1, vb, 16)
        pi[0] += 16
        if l == 0:
            dump("h2", hT[:, 2, 512:1024], [("h", 2, 1)])

        chk('ffn')
        pT3 = pTs.rearrange("p (k n) -> p k n", k=2)

        def ple_tile(wpl, wplk, wpg, wpgk, half, m, t):
            mo = half * 4 + m
            sl = slice(t * TT, (t + 1) * TT)
            bg_ = psum()
            for k in range(8):
                Sx.op("pe", lambda e, k=k, m=m, bg_=bg_: e.matmul(
                    ps[bg_][:, :], wpg[:, k * 512 + m * 128:k * 512 + (m + 1) * 128], xnT[:, k, sl],
                    start=(k == 0), stop=(k == 7)),
                    reads=[wpgk, ("xn", k, t)], writes=[("ps", bg_)])
            bp_ = psum()
            for k in range(2):
                Sx.op("pe", lambda e, k=k, m=m, bp_=bp_: e.matmul(
                    ps[bp_][:, :], wpl[:, k * 512 + m * 128:k * 512 + (m + 1) * 128], pT3[:, k, sl],
                    start=(k == 0), stop=(k == 1)),
                    reads=[wplk, "f_pT"], writes=[("ps", bp_)])
            sg, sgk = tmp()
            Sx.op("act", lambda e, sg=sg, bg_=bg_: e.activation(sg[:, :], ps[bg_][:, :], AF.Sigmoid),
                  reads=[("ps", bg_)], writes=[sgk])
            Sx.op("dve", lambda e, sg=sg, bp_=bp_: e.tensor_tensor(sg[:, :], sg[:, :], ps[bp_][:, :], ALU.mult),
                  reads=[sgk, ("ps", bp_)], writes=[sgk])
            Sx.op("dve", lambda e, sg=sg, mo=mo: e.tensor_tensor(hT[:, mo, sl], hT[:, mo, sl], sg[:, :], ALU.add),
                  reads=[sgk, hkeys(mo, t)], writes=[hkeys(mo, t)])

        wpl, wplk = w_get(pi[0])
        wpg, wpgk = w_get(pi[0] + 1)
        for m in range(4):
            for t in range(NT):
                ple_tile(wpl, wplk, wpg, wpgk, 0, m, t)
        w_done(pi[0])
        w_done(pi[0] + 1)
        pi[0] += 2
        wpl, wplk = w_get(pi[0])
        wpg, wpgk = w_get(pi[0] + 1)
        nxt = l + 1 < n_layers
        for t in range(NT):
            for m in range(4):
                ple_tile(wpl, wplk, wpg, wpgk, 1, m, t)
            if nxt:
                if t >= 1:
                    norm_p2(t - 1, (l + 1) * NV, 0)
                norm_p1(t)
        if nxt:
            norm_p2(NT - 1, (l + 1) * NV, 0)
        w_done(pi[0])
        w_done(pi[0] + 1)
        pi[0] += 2

    try:
        for l in range(n_layers):
            layer_body(l)
    except _Stop:
        pass

    okeys = []
    for k in range(8):
        Sx.dma("sp", "out", out_d[:, k * S:(k + 1) * S], hT[:, k, :], reads=[hkeys(k, t) for t in range(NT)], writes=[("o", k)])
        okeys.append(("o", k))
    okeys += ["dbgo_" + n for n in dbg_d]
    Sx.wait_all("sp", okeys)
    STATS['ins'] = Sx.n_ins
    STATS['waits'] = Sx.n_wait
    return nc


def prepare(I, n_layers):
    pieces = []
    for l in range(n_layers):
        pieces += _pieces_for_layer(I["w_in"][l], I["conv_pw_w"][l], I["w_o"][l], I["w1"][l], I["w2"][l],
                                    I["w_pg"][l], I["w_ple"][l])
    sizes = [p.shape[1] for p in pieces]
    wstream = np.ascontiguousarray(np.concatenate(pieces, axis=1))
    vecs = np.ascontiguousarray(np.concatenate([_vecs(l, I) for l in range(n_layers)], axis=1))
    btab = np.ascontiguousarray(np.concatenate([_bias_tables(I["rel_bias"][l]) for l in range(n_layers)], axis=1))
    consts = _consts()
    in_maps = []
    for b in range(8):
        xT = np.ascontiguousarray(I["x"][b].T.reshape(8, 128, S).transpose(1, 0, 2)).reshape(128, 8 * S)
        pT = np.ascontiguousarray(
            np.stack([I["p"][l, b].T.reshape(2, 128, S).transpose(1, 0, 2).reshape(128, 2 * S) for l in range(n_layers)], axis=1)
        ).reshape(128, n_layers * 2 * S)
        in_maps.append({"xT": xT, "pT": pT, "wstream": wstream, "vecs": vecs, "btab": btab, "consts": consts})
    return sizes, in_maps


def run(I, n_layers=L_FULL, debug=None, trace=False, stop=None):
    I = {k: np.asarray(v, dtype=np.float32) for k, v in I.items()}
    sizes, in_maps = prepare(I, n_layers)
    nc = build_nc(n_layers, sizes, debug=debug, stop=stop)
    res = run_bass_kernel_spmd(nc, in_maps, core_ids=list(range(8)), trace=trace)
    outs = []
    for b in range(8):
        o = res.results[b]["outT"].reshape(128, 8, S).transpose(1, 0, 2).reshape(D, S).T
        outs.append(o)
    out = np.ascontiguousarray(np.stack(outs, axis=0)).astype(np.float32)
    return out, res


def kernel(**inputs):
    out, _ = run(inputs, L_FULL)
    return out
```
